# Optimizing a Trainium2 kernel written in Bass

```python
import math
import jax
import jax.numpy as jnp
from jax import lax
import numpy as np

D_MODEL = 1024
BATCH = 8
SEQ = 4096
DEPTH = 4

GRID_W = 64
CTX_LEN = 256
N_BRANCH = 4
BRANCH_W = 512
ATT_HEADS = 4
ATT_HEAD_DIM = 64
ATT_V_DIM = 2 * ATT_HEAD_DIM
ATT_SCALE = ATT_HEAD_DIM ** -0.5
Q_BLOCK = 128
ROPE_THETA = 10000.0
ROPE_AXIS_PAIRS = ATT_HEAD_DIM // 4
CHUNK = 128
SGU_GROUPS = 4
SGU_GROUP_W = BRANCH_W // SGU_GROUPS
SCONV_K = 3
CCONV_K = 31
D_FF = 4 * D_MODEL
LN_EPS = 1e-5
DEEPNORM_ALPHA = (2 * DEPTH) ** 0.25
DEEPNORM_BETA = (8 * DEPTH) ** -0.25

Q_W = 2 * ATT_HEADS * ATT_HEAD_DIM
K_W = 2 * ATT_HEADS * ATT_HEAD_DIM
V_W = ATT_HEADS * ATT_V_DIM
SGU_IN_W = 2 * BRANCH_W
SCONV_IN_W = 3 * BRANCH_W
CCONV_IN_W = 2 * BRANCH_W
GATE_W = N_BRANCH * D_MODEL
OFF_Q = 0
OFF_K = OFF_Q + Q_W
OFF_V = OFF_K + K_W
OFF_SGU = OFF_V + V_W
OFF_SCONV = OFF_SGU + SGU_IN_W
OFF_CCONV = OFF_SCONV + SCONV_IN_W
OFF_GATE = OFF_CCONV + CCONV_IN_W
N_IN = OFF_GATE + GATE_W

kernel_name = 'hybrid_gated_branch_dit_layers'


def layer_norm(x, g, b):
    xf = x.astype(jnp.float32)
    mu = jnp.mean(xf, axis=-1, keepdims=True)
    var = jnp.mean(jnp.square(xf - mu), axis=-1, keepdims=True)
    y = ((xf - mu) * lax.rsqrt(var + LN_EPS)).astype(x.dtype)
    return y * g + b


def rms_norm(x, g):
    xf = x.astype(jnp.float32)
    y = xf * lax.rsqrt(jnp.mean(xf * xf, axis=-1, keepdims=True) + LN_EPS)
    return y.astype(x.dtype) * g


def modulate(x, shift, scale):
    return x * (1 + scale) + shift


def post_norm(x, update, g, b):
    return layer_norm(DEEPNORM_ALPHA * x + update, g, b)


def depthwise_conv(x, w):
    return lax.conv_general_dilated(
        x, w[:, None, :], window_strides=(1,), padding='SAME',
        dimension_numbers=('NWC', 'WIO', 'NWC'), feature_group_count=x.shape[-1])


def axial_rope_tables(n_tokens):
    rows = n_tokens // GRID_W
    row = jnp.repeat(jnp.arange(rows, dtype=jnp.float32), GRID_W)
    col = jnp.tile(jnp.arange(GRID_W, dtype=jnp.float32), rows)
    inv_freq = ROPE_THETA ** (-jnp.arange(ROPE_AXIS_PAIRS, dtype=jnp.float32) / ROPE_AXIS_PAIRS)
    ang = jnp.concatenate([row[:, None] * inv_freq, col[:, None] * inv_freq], axis=-1)
    return jnp.cos(ang), jnp.sin(ang)


def apply_rope(x, cos, sin):
    xp = x.reshape(x.shape[:-1] + (ATT_HEAD_DIM // 2, 2))
    x0, x1 = xp[..., 0], xp[..., 1]
    cs = cos[:, None, None, :].astype(x.dtype)
    sn = sin[:, None, None, :].astype(x.dtype)
    return jnp.stack([x0 * cs - x1 * sn, x0 * sn + x1 * cs], axis=-1).reshape(x.shape)


def split_q(z):
    b, n = z.shape[:2]
    return z[..., OFF_Q:OFF_K].reshape(b, n, 2, ATT_HEADS, ATT_HEAD_DIM) * ATT_SCALE


def split_kv(zkv):
    b, n = zkv.shape[:2]
    k = zkv[..., :K_W].reshape(b, n, 2, ATT_HEADS, ATT_HEAD_DIM)
    v = zkv[..., K_W:].reshape(b, n, ATT_HEADS, ATT_V_DIM)
    return k, v


def diff_attn_block(q, k, v, lam):
    s = jnp.einsum('bqmhd,bkmhd->bmhqk', q, k).astype(jnp.float32)
    p = jax.nn.softmax(s, axis=-1)
    a = p[:, 0] - lam * p[:, 1]
    return jnp.einsum('bhqk,bkhe->bqhe', a.astype(v.dtype), v)


def diff_attn_latent(q, k, v, lam):
    b, n = q.shape[:2]
    nb = n // Q_BLOCK
    qb = jnp.moveaxis(q.reshape(b, nb, Q_BLOCK, 2, ATT_HEADS, ATT_HEAD_DIM), 1, 0)
    out = lax.map(lambda blk: diff_attn_block(blk, k, v, lam), qb)
    return jnp.moveaxis(out, 0, 1).reshape(b, n, ATT_HEADS, ATT_V_DIM)


def diff_attn_finish(o, g, lam_init):
    b, n = o.shape[:2]
    return (rms_norm(o, g) * (1.0 - lam_init)).reshape(b, n, V_W)


def chunk_sgu(z, ln_g, ln_b, w_s, b_s):
    b, n = z.shape[:2]
    u, v = jnp.split(jax.nn.gelu(z, approximate=False), 2, axis=-1)
    v = layer_norm(v, ln_g, ln_b)
    vb = v.reshape(b, n // CHUNK, CHUNK, SGU_GROUPS, SGU_GROUP_W)
    mixed = jnp.einsum('gpq,bnqgc->bnpgc', w_s, vb) + b_s.T[:, :, None]
    return u * mixed.reshape(b, n, BRANCH_W)


def short_conv(z, w):
    gb, gc, xt = jnp.split(z, 3, axis=-1)
    return gb * depthwise_conv(gc * xt, w)


def conformer_conv(z, w_dw, b_dw, ln_g, ln_b):
    y = jax.nn.glu(z, axis=-1)
    y = depthwise_conv(y, w_dw) + b_dw
    return jax.nn.silu(layer_norm(y, ln_g, ln_b))


def layer_update(x, z, y_b, gate1, shift2, scale2, gate2,
                 sgu_ln_g, sgu_ln_b, sgu_w, sgu_b, sconv_w, cconv_w, cconv_b,
                 cconv_ln_g, cconv_ln_b, w_branch, w_out, ln1_g, ln1_b,
                 w_up, w_down, ln2_g, ln2_b):
    y_a = chunk_sgu(z[..., OFF_SGU:OFF_SCONV], sgu_ln_g, sgu_ln_b, sgu_w, sgu_b)
    y_c = short_conv(z[..., OFF_SCONV:OFF_CCONV], sconv_w)
    y_d = conformer_conv(z[..., OFF_CCONV:OFF_GATE], cconv_w, cconv_b, cconv_ln_g, cconv_ln_b)
    branches = jnp.stack([y_a, y_b, y_c, y_d], axis=2)
    proj = jnp.einsum('bnkw,kwd->bnkd', branches, w_branch)
    gates = jax.nn.sigmoid(z[..., OFF_GATE:].reshape(z.shape[:2] + (N_BRANCH, D_MODEL)))
    mix = jnp.sum(gates * proj, axis=2) @ w_out
    x = post_norm(x, gate1 * mix, ln1_g, ln1_b)
    h = modulate(x, shift2, scale2)
    ffn = jnp.square(jax.nn.relu(h @ w_up)) @ w_down
    return post_norm(x, gate2 * ffn, ln2_g, ln2_b)


def setup_inputs(seed: int = 0) -> dict:
    key = jax.random.key(seed)
    ks = jax.random.split(key, 29)
    f32 = jnp.float32
    L = DEPTH

    def nrm(k, shape, scale):
        return jax.random.normal(k, shape, f32) * scale

    def gain(k, shape):
        return 1.0 + 0.02 * jax.random.normal(k, shape, f32)

    return {
        'x': nrm(ks[0], (BATCH, SEQ, D_MODEL), 1.0),
        'c': nrm(ks[1], (BATCH, D_MODEL), 1.0),
        'ctx': nrm(ks[2], (BATCH, CTX_LEN, D_MODEL), 1.0),
        'c_ctx': nrm(ks[3], (D_MODEL,), 1.0),
        'w_ada': nrm(ks[4], (L, D_MODEL, 6 * D_MODEL), 0.5 * D_MODEL ** -0.5),
        'b_ada': nrm(ks[5], (L, 6 * D_MODEL), 0.02),
        'w_in': nrm(ks[6], (L, D_MODEL, N_IN), D_MODEL ** -0.5),
        'lam_q1': nrm(ks[7], (L, ATT_HEAD_DIM), 0.1),
        'lam_k1': nrm(ks[8], (L, ATT_HEAD_DIM), 0.1),
        'lam_q2': nrm(ks[9], (L, ATT_HEAD_DIM), 0.1),
        'lam_k2': nrm(ks[10], (L, ATT_HEAD_DIM), 0.1),
        'attn_subln_g': gain(ks[11], (L, ATT_V_DIM)),
        'sgu_ln_g': gain(ks[12], (L, BRANCH_W)),
        'sgu_ln_b': nrm(ks[13], (L, BRANCH_W), 0.02),
        'sgu_w': nrm(ks[14], (L, SGU_GROUPS, CHUNK, CHUNK), CHUNK ** -0.5),
        'sgu_b': gain(ks[15], (L, SGU_GROUPS, CHUNK)),
        'sconv_w': nrm(ks[16], (L, SCONV_K, BRANCH_W), SCONV_K ** -0.5),
        'cconv_w': nrm(ks[17], (L, CCONV_K, BRANCH_W), CCONV_K ** -0.5),
        'cconv_b': nrm(ks[18], (L, BRANCH_W), 0.02),
        'cconv_ln_g': gain(ks[19], (L, BRANCH_W)),
        'cconv_ln_b': nrm(ks[20], (L, BRANCH_W), 0.02),
        'w_branch': nrm(ks[21], (L, N_BRANCH, BRANCH_W, D_MODEL), DEEPNORM_BETA * BRANCH_W ** -0.5),
        'w_out': nrm(ks[22], (L, D_MODEL, D_MODEL), DEEPNORM_BETA * D_MODEL ** -0.5),
        'ln1_g': gain(ks[23], (L, D_MODEL)),
        'ln1_b': nrm(ks[24], (L, D_MODEL), 0.02),
        'w_up': nrm(ks[25], (L, D_MODEL, D_FF), D_MODEL ** -0.5),
        'w_down': nrm(ks[26], (L, D_FF, D_MODEL), DEEPNORM_BETA * D_FF ** -0.5),
        'ln2_g': gain(ks[27], (L, D_MODEL)),
        'ln2_b': nrm(ks[28], (L, D_MODEL), 0.02),
    }


def reference(x, c, ctx, c_ctx, w_ada, b_ada, w_in, lam_q1, lam_k1, lam_q2, lam_k2,
              attn_subln_g, sgu_ln_g, sgu_ln_b, sgu_w, sgu_b, sconv_w, cconv_w, cconv_b,
              cconv_ln_g, cconv_ln_b, w_branch, w_out, ln1_g, ln1_b, w_up, w_down,
              ln2_g, ln2_b):
    n_lat = x.shape[1]
    cos, sin = axial_rope_tables(n_lat)
    c_act = jax.nn.silu(c)
    cctx_act = jax.nn.silu(c_ctx)
    xl, xc = x, ctx
    for l in range(DEPTH):
        lam_init = 0.8 - 0.6 * math.exp(-0.3 * l)
        lam = (jnp.exp(jnp.sum((lam_q1[l] * lam_k1[l]).astype(jnp.float32)))
               - jnp.exp(jnp.sum((lam_q2[l] * lam_k2[l]).astype(jnp.float32))) + lam_init)
        mod_lat = jnp.split((c_act @ w_ada[l] + b_ada[l])[:, None, :], 6, axis=-1)
        mod_ctx = jnp.split((cctx_act @ w_ada[l] + b_ada[l])[None, None, :], 6, axis=-1)
        lp = (sgu_ln_g[l], sgu_ln_b[l], sgu_w[l], sgu_b[l], sconv_w[l], cconv_w[l], cconv_b[l],
              cconv_ln_g[l], cconv_ln_b[l], w_branch[l], w_out[l], ln1_g[l], ln1_b[l],
              w_up[l], w_down[l], ln2_g[l], ln2_b[l])

        hc = modulate(xc, mod_ctx[0], mod_ctx[1])
        if l == DEPTH - 1:
            kc, vc = split_kv(hc @ w_in[l, :, OFF_K:OFF_SGU])
            xc_next = xc
        else:
            zc = hc @ w_in[l]
            qc = split_q(zc)
            kc, vc = split_kv(zc[..., OFF_K:OFF_SGU])
            yb_c = diff_attn_finish(diff_attn_block(qc, kc, vc, lam), attn_subln_g[l], lam_init)
            xc_next = layer_update(xc, zc, yb_c, *mod_ctx[2:], *lp)

        hl = modulate(xl, mod_lat[0], mod_lat[1])
        zl = hl @ w_in[l]
        ql = apply_rope(split_q(zl), cos, sin)
        kl, vl = split_kv(zl[..., OFF_K:OFF_SGU])
        kl = apply_rope(kl, cos, sin)
        k_all = jnp.concatenate([kc, kl], axis=1)
        v_all = jnp.concatenate([vc, vl], axis=1)
        yb_l = diff_attn_finish(diff_attn_latent(ql, k_all, v_all, lam), attn_subln_g[l], lam_init)
        xl = layer_update(xl, zl, yb_l, *mod_lat[2:], *lp)
        xc = xc_next
    return xl
```

```python
import math
from contextlib import ExitStack

import numpy as np
import concourse.bass as bass
import concourse.mybir as mybir
from concourse.bass_utils import run_bass_kernel_spmd

F32 = mybir.dt.float32
BF16 = mybir.dt.bfloat16
AF = mybir.ActivationFunctionType
ALU = mybir.AluOpType

D = 1024
DEPTH = 4
SEQ = 4096
CTX = 256
T = SEQ + CTX
NKT = T // 128
GRID_W = 64
D_FF = 4096
LN_EPS = 1e-5
ALPHA = (2 * DEPTH) ** 0.25
ROPE_THETA = 10000.0
OFF_Q, OFF_K, OFF_V, OFF_SGU, OFF_SCONV, OFF_CCONV, OFF_GATE = 0, 512, 1024, 1536, 2560, 4096, 5120
NTILE = 42
TCOLS = 4096
NCOLS = NTILE * TCOLS
SP_COLS = 228
NB = 1920
TP = 4416
TI_K, TI_KS, TI_V, TI_SC, TI_CC = 0, 1, 2, 3, 5
TI_Q, TI_QS, TI_SU, TI_SV, TI_GB, TI_GATE, TI_BR, TI_WO, TI_UP, TI_DN = 7, 8, 9, 10, 11, 12, 20, 24, 26, 34


class Sched:
    CAP = 28000

    def __init__(self):
        self.ops = []
        self.lastw = {}
        self.readers = {}
        self.wait_all = set()

    R1N = ("YT", "U", "VTOK", "VN", "PW", "GW", "HID")
    R2N = ("QTZ", "PT", "RC", "RS", "ACC", "SQ", "G", "MIXG", "ACCst")

    def add(self, eng, fn, r=(), w=(), lane=None):
        i = len(self.ops)
        r = list(r)
        w = list(w)
        if not any(k in ("R1all", "R2all") for k in w):
            for k in r + w:
                nm = k[0] if isinstance(k, tuple) else k
                if nm in self.R1N and "R1all" not in r:
                    r.append("R1all")
                if nm in self.R2N and "R2all" not in r:
                    r.append("R2all")
        deps = set()
        for k in r:
            x = self.lastw.get(k)
            if x is not None:
                deps.add(x)
        for k in w:
            x = self.lastw.get(k)
            if x is not None:
                deps.add(x)
            rd = self.readers.get(k)
            if rd:
                deps.update(rd.values())
        stream = ("L", lane) if lane else ("E", eng)
        for k in w:
            self.lastw[k] = i
            self.readers[k] = {}
        ws = set(w)
        for k in r:
            if k not in ws:
                self.readers.setdefault(k, {})[stream] = i
        self.ops.append([eng, fn, lane, deps])
        return i

    def emit(self, nc, stack, block):
        ops = self.ops
        n = len(ops)
        stream_of = [("L", o[2]) if o[2] else ("E", o[0]) for o in ops]
        waits = [None] * n
        marked = [False] * n
        waited = {}
        for i, (eng, fn, lane, deps) in enumerate(ops):
            best = {}
            for d in deps:
                s = stream_of[d]
                if s == ("E", "pe") and eng == "pe" and lane is None:
                    continue
                if d > best.get(s, -1):
                    best[s] = d
            wl = []
            we = waited.setdefault(eng, {})
            for s, d in best.items():
                if we.get(s, -1) >= d:
                    continue
                we[s] = d
                wl.append(d)
                marked[d] = True
            waits[i] = wl
        sem_of = [None] * n
        cur = {}
        lane_total = {}

        def newsem(tag):
            return stack.enter_context(nc.semaphore("s_%s_%d" % (tag, len(cur_all))))

        cur_all = []
        for i, (eng, fn, lane, deps) in enumerate(ops):
            if lane:
                key = ("L", lane)
                inc = 16
            elif marked[i]:
                key = ("E", eng)
                inc = 1
            else:
                continue
            st = cur.get(key)
            if st is None or st[1] + inc > self.CAP:
                sem = newsem(key[1])
                cur_all.append(sem)
                st = [sem, 0]
                cur[key] = st
            st[1] += inc
            sem_of[i] = (st[0], st[1], inc)
            if lane:
                lane_total[lane] = (st[0], st[1])
        by_eng = {}
        for i, o in enumerate(ops):
            by_eng.setdefault(o[0], []).append(i)

        def run(e, idxs):
            for i in idxs:
                eng, fn, lane, deps = ops[i]
                for d in waits[i]:
                    dl = ops[d][2]
                    if dl in self.wait_all:
                        sem, val = lane_total[dl]
                    else:
                        sem, val, _ = sem_of[d]
                    e.wait_ge(sem, val)
                if fn is None:
                    continue
                ins = fn(e)
                if sem_of[i] is not None:
                    ins.then_inc(sem_of[i][0], sem_of[i][2])

        deco = {"pe": block.tensor, "act": block.scalar, "dve": block.vector,
                "pool": block.gpsimd, "sp": block.sync}
        for name, idxs in by_eng.items():
            def _f(e, idxs=idxs):
                run(e, idxs)
            deco[name](_f)
        return len(cur_all)


def build_program(depth=DEPTH, dbg=False):
    nc = bass.Bass("TRN2", target_bir_lowering=False)
    S = Sched()
    S.wait_all.add("const")

    def din(name, shape, dt=F32):
        return nc.dram_tensor(name, list(shape), dt, kind="ExternalInput").ap()

    xT_in = din("xT", [D, T])
    cs_in = din("cs", [128, 16])
    wbig = din("wbig", [DEPTH, 128, NCOLS])
    wada = din("wada", [DEPTH, 48, 128, 1024])
    smallp_in = din("smallp", [128, DEPTH * SP_COLS])
    bcast_in = din("bcast", [DEPTH, NB])
    sguw_in = din("sguw", [DEPTH, 128, 512])
    ropeC_in = din("ropeC", [128, SEQ])
    ropeS_in = din("ropeS", [128, SEQ])
    ident_in = din("ident", [128, 128])
    outT = nc.dram_tensor("outT", [D, SEQ], F32, kind="ExternalOutput").ap()
    wbf = nc.dram_tensor("wbf", [DEPTH, 128, NCOLS], BF16, kind="Internal").ap()
    xs = nc.dram_tensor("xs", [D, T], F32, kind="Internal").ap()
    cvn = nc.dram_tensor("cvn", [2, 4, 128, TP], F32, kind="Internal").ap()

    with ExitStack() as stack:
        def sb(name, shape, dt=F32):
            return stack.enter_context(nc.sbuf_tensor(name, list(shape), dt))

        KT = sb("KT", [128, 4, T], BF16)
        VA = sb("VA", [128, NKT, 4, 130], BF16)
        W = sb("W", [128, 3, TCOLS], BF16)
        XG = sb("XG", [128, 8, 512])
        H = sb("H", [128, 8, 512], BF16)
        R1 = sb("R1", [128, 8704])
        R2 = sb("R2", [128, 4096])
        TT_ = sb("TT", [128, 8, 512])
        SMALLP = sb("SMALLP", [128, DEPTH * SP_COLS])
        BC = sb("BC", [128, NB])
        SGUW = sb("SGUW", [128, 512], BF16)
        MODT = sb("MODT", [128, 48, 2])
        MS = sb("MS", [128, 2, 2, 48])
        CSL = sb("CSL", [128, 16])
        SC = sb("SC", [128, 8, 2])
        IDENT = sb("IDENT", [128, 128])
        ONES = sb("ONES", [128, 128])
        GSUB = sb("GSUB", [128, 128])
        OT = sb("OT", [128, 3, 128])
        SMALL = sb("SMALL", [128, 128])
        OACC = sb("OACC", [128, 8, 130])
        YB = sb("YB", [128, 4, 128])
        EPS = sb("EPS", [128, 2])
        ZERO = sb("ZERO", [128, 8, 16])
        ZB = sb("ZB", [128, 512], BF16)
        PS = stack.enter_context(nc.psum_tensor("PS", [128, 8, 512], F32))

        R1b = R1.bitcast(BF16)
        R2b = R2.bitcast(BF16)
        YT = R1b[:, 0:8192].rearrange("p (c t) -> p c t", c=16)
        HID = R1b[:, 0:16384].rearrange("p (c t) -> p c t", c=32)
        PW = R1[:, 4096:6272].rearrange("p (c t) -> p c t", c=4)
        GW = R1[:, 6272:8448].rearrange("p (c t) -> p c t", c=4)
        U = R1b[:, 8192:10240].rearrange("p (c t) -> p c t", c=4)
        VTOK = R1[:, 5120:7168].rearrange("p (s t) -> p s t", s=4)
        VN = R1b[:, 14336:16384].rearrange("p (s t) -> p s t", s=4)
        QTZ = R2b[:, 0:4096].rearrange("p (z c t) -> p z c t", z=2, c=4)
        PT = R2b[:, 4096:6144].rearrange("p (s t) -> p s t", s=4)
        RC = R2[:, 3072:3584]
        RS = R2[:, 3584:4096]
        ACC = R2[:, 0:2048].rearrange("p (c t) -> p c t", c=4)
        SQ = R2[:, 2048:4096].rearrange("p (c t) -> p c t", c=4)
        GATES = R2[:, 0:2048].rearrange("p (c t) -> p c t", c=4)
        MIXG = R2b[:, 4096:8192].rearrange("p (c t) -> p c t", c=8)

        def MM(out, lhsT, rhs, start, stop, r, w):
            S.add("pe", lambda e: e.matmul(out, lhsT, rhs, start=start, stop=stop), r, w)

        def ACT(out, in_, func, r, w, bias=None, scale=None, accum=None):
            kw = {}
            if bias is not None:
                kw["bias"] = bias
            if scale is not None:
                kw["scale"] = scale
            if accum is not None:
                kw["accum_out"] = accum
            S.add("act", lambda e: e.activation(out, in_, func, **kw), r, w)

        def TS(eng, out, in0, s1, s2, op0, op1, r, w):
            if op1 is None:
                S.add(eng, lambda e: e.tensor_scalar(out, in0, s1, None, op0), r, w)
            else:
                S.add(eng, lambda e: e.tensor_scalar(out, in0, s1, s2, op0, op1), r, w)

        def TTo(eng, out, in0, in1, op, r, w):
            S.add(eng, lambda e: e.tensor_tensor(out, in0, in1, op), r, w)

        def STT(out, in0, scalar, in1, op0, op1, r, w):
            S.add("dve", lambda e: e.scalar_tensor_tensor(out, in0, scalar, in1, op0, op1), r, w)

        def CP(eng, out, in_, r, w):
            if eng == "act":
                S.add("act", lambda e: e.copy(out, in_), r, w)
            else:
                S.add(eng, lambda e: e.tensor_copy(out, in_), r, w)

        def DMA(eng, out, in_, r, w, lane):
            S.add(eng, lambda e: e.dma_start(out=out, in_=in_), r, w, lane=lane)

        def MEMSET(eng, ap, val, w):
            S.add(eng, lambda e: e.memset(ap, val), (), w)

        def fence(keys):
            S.add("pool", lambda e: e.memset(SMALL[0:1, 63:64], 0.0), (), list(keys) + ["fdummy"])

        bank_ctr = [0]

        def nbank():
            b = bank_ctr[0] % 8
            bank_ctr[0] += 1
            return b

        wslot_ctr = [0]

        def piece_of(ti):
            return ti // 2

        def wload(l, ti):
            s = wslot_ctr[0] % 3
            wslot_ctr[0] += 1
            DMA("sp", W[:, s, :], wbf[l, :, ti * TCOLS:(ti + 1) * TCOLS],
                [("wbf", l, piece_of(ti))], [("W", s)], "w%d" % s)
            return s

        def cast_piece(l, p):
            DMA("pool", wbf[l, :, p * 2 * TCOLS:(p + 1) * 2 * TCOLS],
                wbig[l, :, p * 2 * TCOLS:(p + 1) * 2 * TCOLS],
                [], [("wbf", l, p)], "cast%d_%d" % (p, l % 2))

        def cast_layer(l):
            for p in range(NTILE // 2):
                cast_piece(l, p)

        DMA("sp", SMALLP[:, :], smallp_in[:, :], [], ["SMALLP"], "const")
        DMA("sp", CSL[:, :], cs_in[:, :], [], ["CSL"], "const")
        DMA("sp", IDENT[:, :], ident_in[:, :], [], ["IDENT"], "const")
        MEMSET("dve", ONES[:, :], 1.0, ["ONES"])
        MEMSET("dve", EPS[:, 0:1], LN_EPS, ["EPS"])
        MEMSET("dve", EPS[:, 1:2], LN_EPS / (ALPHA * ALPHA), ["EPS"])
        MEMSET("dve", ZERO[:, :, :], 0.0, ["ZERO"])
        MEMSET("dve", ZB[:, :], 0.0, ["ZB"])
        MEMSET("pool", VA[:, :, :, 128:130], 0.0, ["VA"])
        MEMSET("pool", VA[:, :, :, 128:129], 1.0, ["VA"])
        for a in (0, 272, 288, 4400):
            DMA("sp", cvn[:, :, :, a:a + 16].rearrange("s c p t -> p (s c) t"), ZERO[:, :, :],
                ["ZERO"], [("cvnpad", a)], "const")
        ACT(SC[:, :, :], CSL[:, :].rearrange("p (c t) -> p c t", t=2), AF.Silu, ["CSL"], ["SC"])
        cast_layer(0)

        groups = [(0, CTX, True)] + [(CTX + 512 * g, 512, False) for g in range(SEQ // 512)]

        def mod_piece(lm, jj, use_w):
            if use_w:
                s_ = wslot_ctr[0] % 3
                wslot_ctr[0] += 1
                wa = W[:, s_, :].bitcast(F32)
                wkeys = [("W", s_)]
                lane = "w%d" % s_
            else:
                slot = jj % 2
                wa = TT_[:, slot * 4:(slot + 1) * 4, :].rearrange("p a b -> p (a b)")
                wkeys = [("T", slot * 4 + i) for i in range(4)]
                lane = "wa%d" % slot
            DMA("sp", wa.rearrange("p (j c) -> p j c", j=2),
                wada[lm, 2 * jj:2 * jj + 2, :, :].rearrange("j p c -> p j c"), [], wkeys, lane)
            mb = nbank()
            for j2 in range(2):
                for kc in range(8):
                    MM(PS[:, mb, 2 * j2:2 * j2 + 2], wa[:, j2 * 1024 + kc * 128: j2 * 1024 + (kc + 1) * 128],
                       SC[:, kc, :], kc == 0, kc == 7, wkeys + ["SC"], [("ps", mb)])
            for t in range(2):
                TTo("dve", MODT[:, 2 * jj:2 * jj + 2, t], PS[:, mb, 0:4].rearrange("p (j t) -> p j t", t=2)[:, :, t],
                    SMALLP[:, lm * SP_COLS + 2 * jj: lm * SP_COLS + 2 * jj + 2], ALU.add,
                    [("ps", mb), "SMALLP"], [("MODT", jj, t)])

        def mod_derive(lm):
            par = lm % 2
            b0 = lm * SP_COLS
            mk = [("MODT", jj, t) for jj in range(24) for t in range(2)]
            for t in range(2):
                m = MODT[:, :, t]
                ms = MS[:, par, t, :]
                kms = [("MS", par)]
                CP("dve", ms[:, 0:8], m[:, 0:8], mk, kms)
                TS("dve", ms[:, 8:16], m[:, 8:16], 1.0, None, ALU.add, None, mk, kms)
                TS("dve", ms[:, 16:24], m[:, 16:24], 1.0 / ALPHA, None, ALU.mult, None, mk, kms)
                TS("dve", SMALL[:, 8:16], m[:, 32:40], 1.0, None, ALU.add, None, mk, ["SM_sc2"])
                TTo("dve", ms[:, 24:32], SMALLP[:, b0 + 48:b0 + 56], SMALL[:, 8:16], ALU.mult, ["SM_sc2", "SMALLP"], kms)
                TTo("dve", SMALL[:, 16:24], SMALLP[:, b0 + 56:b0 + 64], SMALL[:, 8:16], ALU.mult, ["SM_sc2", "SMALLP"], ["SM_b2"])
                TTo("dve", ms[:, 32:40], SMALL[:, 16:24], m[:, 24:32], ALU.add, ["SM_b2"] + mk, kms)
                TS("dve", ms[:, 40:48], m[:, 40:48], 1.0 / ALPHA, None, ALU.mult, None, mk, kms)

        def cvcol(t0):
            return t0 + 16 if t0 < CTX else t0 + 48

        def xkeys(t0, gt):
            return [("x", t0)]

        for l in range(depth):
            last = (l == DEPTH - 1)
            lam_init = 0.8 - 0.6 * math.exp(-0.3 * l)
            spb = l * SP_COLS
            xsrc = xT_in if l == 0 else xs

            def sp(c0, n=1, spb=spb):
                return SMALLP[:, spb + c0: spb + c0 + n]

            DMA("sp", BC[:, :], bcast_in[l:l + 1, :].partition_broadcast(128), [], ["BC"], "bc")
            DMA("sp", TT_[:, 0, :], sguw_in[l, :, :], [], [("T", 0)], "sguw")
            CP("dve", SGUW[:, :], TT_[:, 0, :], [("T", 0)], ["SGUW"])
            TS("dve", GSUB[:, :], BC[:, 1536:1664], 1.0 - lam_init, None, ALU.mult, None, ["BC"], ["GSUB"])
            TTo("dve", OT[:, 0, 0:64], BC[:, 1664:1728], BC[:, 1728:1792], ALU.mult, ["BC"], ["OT0"])
            TTo("dve", OT[:, 0, 64:128], BC[:, 1792:1856], BC[:, 1856:1920], ALU.mult, ["BC"], ["OT0"])
            S.add("dve", lambda e: e.tensor_reduce(SMALL[:, 0:2], OT[:, 0, :].rearrange("p (a b) -> p a b", a=2),
                                                   mybir.AxisListType.X, ALU.add), ["OT0"], ["SM_lam"])
            ACT(SMALL[:, 2:4], SMALL[:, 0:2], AF.Exp, ["SM_lam"], ["SM_lam2"])
            TTo("dve", SMALL[:, 4:5], SMALL[:, 2:3], SMALL[:, 3:4], ALU.subtract, ["SM_lam2"], ["SM_lam3"])
            TS("dve", SMALL[:, 5:6], SMALL[:, 4:5], -1.0, -lam_init, ALU.mult, ALU.add, ["SM_lam3"], ["NEGLAM"])
            NEGLAM = SMALL[:, 5:6]

            par = l % 2
            if l == 0:
                for jj in range(24):
                    mod_piece(0, jj, False)
                mod_derive(0)

            def load_h(t0, gt, isctx):
                tt = 1 if isctx else 0
                DMA("sp", XG[:, :, 0:gt], xsrc.rearrange("(c p) t -> p c t", p=128)[:, :, t0:t0 + gt],
                    ([("x", t0, j) for j in range(8)] if l > 0 else []), [("XG", j) for j in range(8)], "xg")
                for kc in range(8):
                    ACT(H[:, kc, 0:gt], XG[:, kc, 0:gt], AF.Identity, [("XG", kc), ("MS", par)], [("H", kc)],
                        bias=MS[:, par, tt, kc:kc + 1], scale=MS[:, par, tt, 8 + kc:9 + kc])

            def proj_st(slot, ocl, gt, kcn=8, src=None, srck=None):
                b = nbank()
                for kc in range(kcn):
                    rhs = H[:, kc, 0:gt] if src is None else src(kc)
                    MM(PS[:, b, 0:gt], W[:, slot, (ocl * kcn + kc) * 128:(ocl * kcn + kc + 1) * 128], rhs,
                       kc == 0, kc == kcn - 1,
                       [("W", slot), (("H", kc) if srck is None else srck(kc))], [("ps", b)])
                return b

            def rope_to(dst, dstkeys, bq, bs, gt, slot):
                t1 = TT_[:, 4 + (slot % 2) * 2, 0:gt]
                t2 = TT_[:, 5 + (slot % 2) * 2, 0:gt]
                k1 = ("T", 4 + (slot % 2) * 2)
                k2 = ("T", 5 + (slot % 2) * 2)
                TTo("dve", t1, PS[:, bq, 0:gt], RC[:, 0:gt], ALU.mult, [("ps", bq), "RC"], [k1])
                TTo("dve", t2, PS[:, bs, 0:gt], RS[:, 0:gt], ALU.mult, [("ps", bs), "RS"], [k2])
                TTo("pool" if slot % 2 == 0 else "dve", dst, t1, t2, ALU.add, [k1, k2], dstkeys)

            def load_rope(t0, gt):
                DMA("sp", RC[:, 0:gt], ropeC_in[:, t0 - CTX:t0 - CTX + gt], [], ["RC"], "ropeC")
                DMA("sp", RS[:, 0:gt], ropeS_in[:, t0 - CTX:t0 - CTX + gt], [], ["RS"], "ropeS")

            fence(["R1all", "R2all"])
            for (t0, gt, isctx) in groups:
                nqb = gt // 128
                load_h(t0, gt, isctx)
                if not isctx:
                    load_rope(t0, gt)
                sk = wload(l, TI_K)
                if not isctx:
                    sks = wload(l, TI_KS)
                for c in range(4):
                    bq = proj_st(sk, c, gt)
                    if isctx:
                        CP("act", KT[:, c, t0:t0 + gt], PS[:, bq, 0:gt], [("ps", bq)], [("KT", c, t0)])
                    else:
                        bs = proj_st(sks, c, gt)
                        rope_to(KT[:, c, t0:t0 + gt], [("KT", c, t0)], bq, bs, gt, c)
                sv = wload(l, TI_V)
                for tt in range(nqb):
                    b = nbank()
                    for kc in range(8):
                        MM(PS[:, b, :], H[:, kc, tt * 128:(tt + 1) * 128], W[:, sv, kc * 512:(kc + 1) * 512],
                           kc == 0, kc == 7, [("W", sv), ("H", kc)], [("ps", b)])
                    kt = t0 // 128 + tt
                    CP("act", VA[:, kt, :, 0:128], PS[:, b, :].rearrange("p (h e) -> p h e", h=4),
                       [("ps", b)], [("VA", kt)])
                if (not isctx) or (not last):
                    for half in range(2):
                        s = wload(l, TI_SC + half)
                        for cc in range(2):
                            c = half * 2 + cc
                            bg = proj_st(s, cc * 2, gt)
                            bx = proj_st(s, cc * 2 + 1, gt)
                            CP("act", TT_[:, 4 + cc, 0:gt], PS[:, bg, 0:gt], [("ps", bg)], [("T", 4 + cc)])
                            TTo("dve", ACC[:, c, 0:gt], PS[:, bx, 0:gt], TT_[:, 4 + cc, 0:gt], ALU.mult,
                                [("ps", bx), ("T", 4 + cc)], ["ACCst"])
                    DMA("pool", cvn[0, :, :, cvcol(t0):cvcol(t0) + gt].rearrange("c p t -> p c t"), ACC[:, :, 0:gt],
                        ["ACCst"], [("cv", 0, t0)], "pst")
                    for half in range(2):
                        s = wload(l, TI_CC + half)
                        for cc in range(2):
                            c = half * 2 + cc
                            ba = proj_st(s, cc * 2, gt)
                            bb = proj_st(s, cc * 2 + 1, gt)
                            ACT(TT_[:, 6 + cc, 0:gt], PS[:, bb, 0:gt], AF.Sigmoid, [("ps", bb)], [("T", 6 + cc)])
                            TTo("dve", TT_[:, c, 0:gt], PS[:, ba, 0:gt], TT_[:, 6 + cc, 0:gt], ALU.mult,
                                [("ps", ba), ("T", 6 + cc)], ["GLUst"] + [("T", c)])
                    DMA("pool", cvn[1, :, :, cvcol(t0):cvcol(t0) + gt].rearrange("c p t -> p c t"), TT_[:, 0:4, 0:gt],
                        ["GLUst"] + [("T", c) for c in range(4)], [("cv", 1, t0)], "gst")

            def ln_stats(vsrc, vkeys, nch, gt, nfeat, epscol, sqslot_base):
                b1 = nbank()
                b2 = nbank()
                for j in range(nch):
                    sq = TT_[:, sqslot_base + (j % 2), 0:gt]
                    kq = ("T", sqslot_base + (j % 2))
                    ACT(sq, vsrc(j), AF.Square, [vkeys(j)], [kq])
                    MM(PS[:, b1, 0:gt], ONES[:, :], vsrc(j), j == 0, j == nch - 1, ["ONES", vkeys(j)], [("ps", b1)])
                    MM(PS[:, b2, 0:gt], ONES[:, :], sq, j == 0, j == nch - 1, ["ONES", kq], [("ps", b2)])
                ta = TT_[:, 2, 0:gt]
                tb = TT_[:, 3, 0:gt]
                ka, kb = ("T", 2), ("T", 3)
                ACT(ta, PS[:, b1, 0:gt], AF.Copy, [("ps", b1)], [ka], scale=1.0 / nfeat)
                TTo("dve", tb, ta, ta, ALU.mult, [ka], [kb])
                STT(tb, PS[:, b2, 0:gt], 1.0 / nfeat, tb, ALU.mult, ALU.subtract, [("ps", b2), kb], [kb])
                ACT(tb, tb, AF.Sqrt, [kb, "EPS"], [kb], bias=EPS[:, epscol:epscol + 1])
                S.add("dve", lambda e: e.reciprocal(tb, tb), [kb], [kb])
                STT(ta, ta, -1.0, tb, ALU.mult, ALU.mult, [ka, kb], [ka])
                return tb, ta, kb, ka

            def ln2_stats(gt):
                b1 = nbank()
                b2 = nbank()
                for j in range(8):
                    sq = TT_[:, j % 2, 0:gt]
                    kq = ("T", j % 2)
                    ACT(sq, XG[:, j, 0:gt], AF.Square, [("XG", j)], [kq])
                    MM(PS[:, b1, 0:gt], ONES[:, :], XG[:, j, 0:gt], j == 0, j == 7, ["ONES", ("XG", j)], [("ps", b1)])
                    MM(PS[:, b2, 0:gt], ONES[:, :], sq, j == 0, j == 7, ["ONES", kq], [("ps", b2)])
                TS("dve", TT_[:, 2, 0:gt], PS[:, b1, 0:gt], 1.0 / 1024.0, None, ALU.mult, None, [("ps", b1)], [("T", 2)])
                TS("dve", TT_[:, 3, 0:gt], PS[:, b2, 0:gt], 1.0 / 1024.0, None, ALU.mult, None, [("ps", b2)], [("T", 3)])

            def ln2_rowmath(gt):
                ta = TT_[:, 2, 0:gt]
                tb = TT_[:, 3, 0:gt]
                tc = TT_[:, 0, 0:gt]
                ka, kb, kc_ = ("T", 2), ("T", 3), ("T", 0)
                TTo("dve", tc, ta, ta, ALU.mult, [ka], [kc_])
                TTo("dve", tb, tb, tc, ALU.subtract, [kb, kc_], [kb])
                ACT(tb, tb, AF.Sqrt, [kb, "EPS"], [kb], bias=EPS[:, 1:2])
                S.add("dve", lambda e: e.reciprocal(tb, tb), [kb], [kb])
                STT(ta, ta, -1.0, tb, ALU.mult, ALU.mult, [ka, kb], [ka])
                return tb, ta, kb, ka

            def prefetch_h(t0n, gtn, isctxn):
                ttn = 1 if isctxn else 0
                for kc in range(8):
                    slot = kc % 2
                    DMA("pool", TT_[:, slot, 0:gtn], xsrc[kc * 128:(kc + 1) * 128, t0n:t0n + gtn],
                        ([("x", t0n, kc)] if l > 0 else []), [("T", slot)], "xp%d" % slot)
                    ACT(H[:, kc, 0:gtn], TT_[:, slot, 0:gtn], AF.Identity, [("T", slot), ("MS", par)], [("H", kc)],
                        bias=MS[:, par, ttn, kc:kc + 1], scale=MS[:, par, ttn, 8 + kc:9 + kc])

            p2groups = [g_ for g_ in groups if not (g_[2] and last)]
            deferred = [None]
            for gi, (t0, gt, isctx) in enumerate(p2groups):
                nqb = gt // 128
                tt_ = 1 if isctx else 0
                fence(["R1all", "R2all"])
                if gi == 0:
                    load_h(t0, gt, isctx)
                sq_ = wload(l, TI_Q)
                if not isctx:
                    sqs = wload(l, TI_QS)
                w0 = cvcol(t0) - 16
                nbr = [("cv", 0, tn) for (tn, gn, cn) in groups if cn == isctx and abs(tn - t0) <= 512]
                DMA("sp", PW[:, :, 0:gt + 32], cvn[0, :, :, w0:w0 + gt + 32].rearrange("c p t -> p c t"),
                    nbr + [("cvnpad", a) for a in (0, 272, 288, 4400)], ["PW"], "pw")
                nbr = [("cv", 1, tn) for (tn, gn, cn) in groups if cn == isctx and abs(tn - t0) <= 512]
                DMA("sp", GW[:, :, 0:gt + 32], cvn[1, :, :, w0:w0 + gt + 32].rearrange("c p t -> p c t"),
                    nbr + [("cvnpad", a) for a in (0, 272, 288, 4400)], ["GW"], "gw")
                if not isctx:
                    load_rope(t0, gt)
                MEMSET("pool", QTZ[0:64, 1, :, 0:gt], 0.0, ["QTZ"])
                MEMSET("dve", QTZ[64:128, 0, :, 0:gt], 0.0, ["QTZ"])
                for c in range(4):
                    bq = proj_st(sq_, c, gt)
                    if isctx:
                        CP("act", QTZ[0:64, 0, c, 0:gt], PS[0:64, bq, 0:gt], [("ps", bq)], ["QTZ"])
                        CP("act", QTZ[64:128, 1, c, 0:gt], PS[64:128, bq, 0:gt], [("ps", bq)], ["QTZ"])
                    else:
                        bs = proj_st(sqs, c, gt)
                        t1 = TT_[:, 4 + (c % 2) * 2, 0:gt]
                        t2 = TT_[:, 5 + (c % 2) * 2, 0:gt]
                        k1 = ("T", 4 + (c % 2) * 2)
                        k2 = ("T", 5 + (c % 2) * 2)
                        TTo("dve", t1, PS[:, bq, 0:gt], RC[:, 0:gt], ALU.mult, [("ps", bq), "RC"], [k1])
                        TTo("dve", t2, PS[:, bs, 0:gt], RS[:, 0:gt], ALU.mult, [("ps", bs), "RS"], [k2])
                        TTo("pool", QTZ[0:64, 0, c, 0:gt], t1[0:64, :], t2[0:64, :], ALU.add, [k1, k2],
                            ["QTZ"])
                        TTo("dve", QTZ[64:128, 1, c, 0:gt], t1[64:128, :], t2[64:128, :], ALU.add, [k1, k2],
                            ["QTZ"])
                if deferred[0] is not None:
                    deferred[0]()
                    deferred[0] = None
                if l + 1 < depth and not isctx:
                    g_ = (t0 - CTX) // 512
                    for p_ in range(3 * g_, min(3 * g_ + 3, NTILE // 2)):
                        cast_piece(l + 1, p_)
                kts = list(range(0, CTX // 128)) if isctx else list(range(NKT))
                sctr = 0
                tctr = [0]
                pending = [None]

                def do_transposes(hh):
                    for qb in range(nqb):
                        tb_ = 4 + tctr[0] % 4
                        tctr[0] += 1
                        S.add("pe", lambda e, tb_=tb_, qb=qb: e.transpose(PS[:, tb_, 0:128], YB[:, qb, :], IDENT[:, :]),
                              [("YB", qb), "IDENT"], [("ps", tb_)])
                        CP("act", YT[:, 4 + hh, qb * 128:(qb + 1) * 128], PS[:, tb_, 0:128], [("ps", tb_)],
                           [("YT", 4 + hh)])
                for h in range(4):
                    hp = h % 2
                    cs_ = [h // 2, 2 + h // 2]
                    kx = ("XG", 2 * h)
                    TS("dve", XG[:, 2 * h, 0:gt], GW[:, h, 1:1 + gt], sp(92 + h * 31), sp(216 + h), ALU.mult, ALU.add,
                       ["GW", "SMALLP", kx], [kx])
                    for tap in range(1, 31):
                        STT(XG[:, 2 * h, 0:gt], GW[:, h, 1 + tap:1 + tap + gt], sp(92 + h * 31 + tap), XG[:, 2 * h, 0:gt],
                            ALU.mult, ALU.add, ["GW", "SMALLP", kx], [kx])
                    for c_ in (range(4) if h == 3 else ()):
                        ky = ("XG", 2 * c_ + 1)
                        TS("dve", XG[:, 2 * c_ + 1, 0:gt], PW[:, c_, 15:15 + gt], sp(80 + c_ * 3), None, ALU.mult, None,
                           ["PW", "SMALLP", ky], [ky])
                        for tap in (1, 2):
                            STT(XG[:, 2 * c_ + 1, 0:gt], PW[:, c_, 15 + tap:15 + tap + gt], sp(80 + c_ * 3 + tap),
                                XG[:, 2 * c_ + 1, 0:gt], ALU.mult, ALU.add, ["PW", "SMALLP", ky], [ky])

                    def accap(a, ncol=129):
                        return PS[:, a // 2, (a % 2) * 256:(a % 2) * 256 + ncol]

                    def qk(kt, sidx):
                        for m in range(2):
                            sb_ = 4 + (sidx + m) % 4
                            MM(PS[:, sb_, 0:gt], KT[:, cs_[m], kt * 128:(kt + 1) * 128], QTZ[:, hp, cs_[m], 0:gt],
                               True, True, [("KT", cs_[m], (0 if kt * 128 < CTX else CTX + ((kt * 128 - CTX) // 512) * 512)), "QTZ"],
                               [("ps", sb_)])

                    def ex(kt, sidx):
                        for m in range(2):
                            sb_ = 4 + (sidx + m) % 4
                            pslot = (sidx + m) % 4
                            ACT(PT[:, pslot, 0:gt], PS[:, sb_, 0:gt], AF.Exp, [("ps", sb_)], [("PT", pslot)],
                                scale=0.125)

                    def av(kt, sidx, first, lastk):
                        for m in range(2):
                            pslot = (sidx + m) % 4
                            for qb in range(nqb):
                                a = m * nqb + qb
                                MM(accap(a), PT[:, pslot, qb * 128:(qb + 1) * 128], VA[:, kt, h, 0:129],
                                   False, lastk, [("PT", pslot), ("VA", kt), "VA"], [("ps", a // 2)])

                    sid = {}
                    for i, kt in enumerate(kts):
                        sid[kt] = sctr
                        sctr += 2
                    for a_ in range(2 * nqb):
                        MM(accap(a_), ZB[:, 0:128], ZB[:, 0:129], True, False, ["ZB"], [("ps", a_ // 2)])
                    qk(kts[0], sid[kts[0]])
                    ex(kts[0], sid[kts[0]])
                    for i, kt in enumerate(kts):
                        if i + 1 < len(kts):
                            qk(kts[i + 1], sid[kts[i + 1]])
                            ex(kts[i + 1], sid[kts[i + 1]])
                        av(kt, sid[kt], i == 0, i == len(kts) - 1)
                        if i == min(6, len(kts) - 1) and pending[0] is not None:
                            pending[0]()
                            pending[0] = None
                    if h > 0:
                        do_transposes(h - 1)
                    nacc = 2 * nqb
                    CP("act", OACC[:, 0:4, 0:129],
                       PS[:, 0:2, :].rearrange("p b (o c) -> p (b o) c", o=2)[:, :, 0:129],
                       [("ps", 0), ("ps", 1)], [("OACC", 0)])
                    if nqb == 4:
                        CP("act", OACC[:, 4:8, 0:129],
                           PS[:, 2:4, :].rearrange("p b (o c) -> p (b o) c", o=2)[:, :, 0:129],
                           [("ps", 2), ("ps", 3)], [("OACC", 1)])
                    okeys = [("OACC", 0), ("OACC", 1)] if nqb == 4 else [("OACC", 0)]
                    sums = OACC[:, 0:nacc, 128:129]
                    S.add("dve", lambda e, sums=sums, nacc=nacc: e.reciprocal(
                        SMALL[:, 24:24 + nacc].rearrange("p (a o) -> p a o", o=1), sums), okeys, ["SM_rc"])
                    TS("dve", SMALL[:, 32:32 + nqb], SMALL[:, 24 + nqb:24 + 2 * nqb], NEGLAM, None, ALU.mult, None,
                       ["SM_rc", "NEGLAM"], ["SM_s1"])
                    for qb in range(nqb):
                        a0, a1 = qb, nqb + qb
                        k0, k1_ = ("OACC", a0 // 4), ("OACC", a1 // 4)
                        TS("dve", OT[:, 0, :], OACC[:, a0, 0:128], SMALL[:, 24 + qb:25 + qb], None, ALU.mult, None,
                           [k0, "SM_rc"], ["OT0"])
                        STT(OACC[:, a1, 0:128], OACC[:, a1, 0:128], SMALL[:, 32 + qb:33 + qb], OT[:, 0, :],
                            ALU.mult, ALU.add, [k1_, "SM_s1", "OT0"], [k1_])
                        S.add("dve", lambda e, a1=a1, qb=qb: e.scalar_tensor_tensor(
                            OT[:, 1, :], OACC[:, a1, 0:128], 1.0, OACC[:, a1, 0:128], ALU.mult, ALU.mult,
                            accum_out=SMALL[:, 40 + qb:41 + qb]), [k1_], ["OT1", ("SM_ss", qb)])

                    def part2(nqb=nqb):
                        ACT(SMALL[:, 44:44 + nqb], SMALL[:, 40:40 + nqb], AF.Sqrt,
                            [("SM_ss", q_) for q_ in range(nqb)] + ["EPS"], ["SM_sd"], bias=EPS[:, 0:1], scale=1.0 / 128.0)
                        S.add("dve", lambda e: e.reciprocal(SMALL[:, 48:48 + nqb], SMALL[:, 44:44 + nqb]), ["SM_sd"], ["SM_rr"])
                        for qb in range(nqb):
                            a1 = nqb + qb
                            STT(YB[:, qb, :], OACC[:, a1, 0:128], SMALL[:, 48 + qb:49 + qb], GSUB[:, :], ALU.mult, ALU.mult,
                                [("OACC", a1 // 4), "SM_rr", "GSUB"], [("YB", qb)])
                    pending[0] = part2
                pending[0]()
                pending[0] = None
                do_transposes(3)
                fence(["R1all", "R2all"])
                sv = wload(l, TI_SV)
                for tt in range(nqb):
                    b = nbank()
                    for kc in range(8):
                        MM(PS[:, b, :], H[:, kc, tt * 128:(tt + 1) * 128], W[:, sv, kc * 512:(kc + 1) * 512],
                           kc == 0, kc == 7, [("W", sv), ("H", kc)], [("ps", b)])
                    vs = tt
                    kv = ("VTOK", vs)
                    sm = 64 + tt * 12
                    ks = "SMv%d" % tt
                    ACT(VTOK[:, vs, :], PS[:, b, :], AF.Gelu, [("ps", b)], [kv])
                    S.add("dve", lambda e, vs=vs, sm=sm: e.bn_stats(SMALL[:, sm:sm + 6], VTOK[:, vs, :]), [kv], [ks + "a"])
                    S.add("dve", lambda e, sm=sm: e.bn_aggr(SMALL[:, sm + 6:sm + 8], SMALL[:, sm:sm + 6]), [ks + "a"], [ks + "b"])
                    ACT(SMALL[:, sm + 8:sm + 9], SMALL[:, sm + 7:sm + 8], AF.Sqrt, [ks + "b", "EPS"], [ks + "c"], bias=EPS[:, 0:1])
                    S.add("dve", lambda e, sm=sm: e.reciprocal(SMALL[:, sm + 9:sm + 10], SMALL[:, sm + 8:sm + 9]), [ks + "c"], [ks + "d"])
                    TS("dve", VTOK[:, vs, :], VTOK[:, vs, :], SMALL[:, sm + 6:sm + 7], SMALL[:, sm + 9:sm + 10],
                       ALU.subtract, ALU.mult, [kv, ks + "b", ks + "d"], [kv])
                    TTo("pool", VTOK[:, vs, :], VTOK[:, vs, :], BC[:, 0:512], ALU.mult, [kv, "BC"], [kv])
                    TTo("pool", VN[:, vs, :], VTOK[:, vs, :], BC[:, 512:1024], ALU.add, [kv, "BC"], [("VN", vs)])
                if l + 1 < depth and not isctx:
                    g_ = (t0 - CTX) // 512
                    for jj in range(3 * g_, 3 * g_ + 3):
                        mod_piece(l + 1, jj, True)
                sg = wload(l, TI_GB)
                for c in range(4):
                    b = proj_st(sg, c, gt)
                    TTo("dve", YT[:, 8 + c, 0:gt], PS[:, b, 0:gt], XG[:, 2 * c + 1, 0:gt], ALU.mult,
                        [("ps", b), ("XG", 2 * c + 1)], [("YT", 8 + c)])
                A_, B_, kA, kB = ln_stats(lambda j: XG[:, 2 * j, 0:gt], lambda j: ("XG", 2 * j), 4, gt, 512.0, 0, 0)
                for c in range(4):
                    TTo("pool", SQ[:, c, 0:gt], XG[:, 2 * c, 0:gt], A_, ALU.mult, [("XG", 2 * c), kA], [("SQ", c)])
                    TTo("dve", SQ[:, c, 0:gt], SQ[:, c, 0:gt], B_, ALU.add, [("SQ", c), kB], [("SQ", c)])
                    ACT(YT[:, 12 + c, 0:gt], SQ[:, c, 0:gt], AF.Silu, [("SQ", c), "SMALLP"], [("YT", 12 + c)],
                        bias=sp(224 + c), scale=sp(220 + c))
                DMA("pool", XG[:, :, 0:gt], xsrc.rearrange("(c p) t -> p c t", p=128)[:, :, t0:t0 + gt],
                    ([("x", t0, j) for j in range(8)] if l > 0 else []), [("XG", j) for j in range(8)], "xg2")
                su = wload(l, TI_SU)
                for c in range(4):
                    b = proj_st(su, c, gt)
                    ACT(U[:, c, 0:gt], PS[:, b, 0:gt], AF.Gelu, [("ps", b)], [("U", c)])
                for tt in range(nqb):
                    vs = tt
                    b2 = nbank()
                    for g in range(4):
                        MM(PS[:, b2, g * 128:(g + 1) * 128], VN[:, vs, g * 128:(g + 1) * 128],
                           SGUW[:, g * 128:(g + 1) * 128], True, True, [("VN", vs), "SGUW"], [("ps", b2)])
                    t1 = TT_[:, 4 + tt % 2, :]
                    k1 = ("T", 4 + tt % 2)
                    TTo("dve", t1, PS[:, b2, :], BC[:, 1024:1536], ALU.add, [("ps", b2), "BC"], [k1])
                    TTo("pool", YT[:, 0:4, tt * 128:(tt + 1) * 128], t1.rearrange("p (g t) -> p g t", g=4),
                        U[:, :, tt * 128:(tt + 1) * 128], ALU.mult, [k1] + [("U", c) for c in range(4)],
                        [("YT", c) for c in range(4)])
                fence(["R2all"])
                for j in range(8):
                    sgt = wload(l, TI_GATE + j)
                    if j % 2 == 0:
                        sbr = wload(l, TI_BR + j // 2)
                    for k in range(4):
                        bg = proj_st(sgt, k, gt)
                        ACT(GATES[:, k, 0:gt], PS[:, bg, 0:gt], AF.Sigmoid, [("ps", bg)], [("G", k)])
                    for k in range(4):
                        bp = nbank()
                        for kc in range(4):
                            col = (j % 2) * 2048 + (k * 4 + kc) * 128
                            MM(PS[:, bp, 0:gt], W[:, sbr, col:col + 128], YT[:, k * 4 + kc, 0:gt], kc == 0, kc == 3,
                               [("W", sbr), ("YT", k * 4 + kc)], [("ps", bp)])
                        TTo("dve", TT_[:, 4 + k, 0:gt], PS[:, bp, 0:gt], GATES[:, k, 0:gt], ALU.mult,
                            [("ps", bp), ("G", k)], [("T", 4 + k)])
                    TTo("pool", TT_[:, 4, 0:gt], TT_[:, 4, 0:gt], TT_[:, 5, 0:gt], ALU.add, [("T", 4), ("T", 5)], [("T", 4)])
                    TTo("pool", TT_[:, 6, 0:gt], TT_[:, 6, 0:gt], TT_[:, 7, 0:gt], ALU.add, [("T", 6), ("T", 7)], [("T", 6)])
                    TTo("pool", MIXG[:, j, 0:gt], TT_[:, 4, 0:gt], TT_[:, 6, 0:gt], ALU.add, [("T", 4), ("T", 6)],
                        [("MIXG", j)])
                for half in range(2):
                    so = wload(l, TI_WO + half)
                    for jj in range(4):
                        j = half * 4 + jj
                        b = proj_st(so, jj, gt, 8, lambda kc: MIXG[:, kc, 0:gt], lambda kc: ("MIXG", kc))
                        STT(XG[:, j, 0:gt], PS[:, b, 0:gt], MS[:, par, tt_, 16 + j:17 + j], XG[:, j, 0:gt], ALU.mult, ALU.add,
                            [("ps", b), ("MS", par), ("XG", j)], [("XG", j)])
                A_, B_, kA, kB = ln_stats(lambda j: XG[:, j, 0:gt], lambda j: ("XG", j), 8, gt, 1024.0, 1, 0)
                fence(["R1all"])
                for j in range(8):
                    t1 = TT_[:, 4 + j % 4, 0:gt]
                    k1 = ("T", 4 + j % 4)
                    TTo("pool", t1, XG[:, j, 0:gt], A_, ALU.mult, [("XG", j), kA], [k1])
                    TTo("dve", t1, t1, B_, ALU.add, [k1, kB], [k1])
                    ACT(XG[:, j, 0:gt], t1, AF.Identity, [k1, "SMALLP"], [("XG", j)], bias=sp(56 + j), scale=sp(48 + j))
                    ACT(H[:, j, 0:gt], t1, AF.Identity, [k1, ("MS", par)], [("H", j)],
                        bias=MS[:, par, tt_, 32 + j:33 + j], scale=MS[:, par, tt_, 24 + j:25 + j])
                def relu2(i, b):
                    t1 = TT_[:, 4 + i % 4, 0:gt]
                    k1 = ("T", 4 + i % 4)
                    ACT(t1, PS[:, b, 0:gt], AF.Relu, [("ps", b)], [k1])
                    TTo("pool", HID[:, i, 0:gt], t1, t1, ALU.mult, [k1], [("HID", i)])

                s01 = [wload(l, TI_UP + 0), wload(l, TI_UP + 1)]
                for kc in range(8):
                    for i in range(8):
                        col = ((i % 4) * 8 + kc) * 128
                        MM(PS[:, i, 0:gt], W[:, s01[i // 4], col:col + 128], H[:, kc, 0:gt], kc == 0, kc == 7,
                           [("W", s01[i // 4]), ("H", kc)], [("ps", i)])
                for i in range(8):
                    relu2(i, i)
                for i4 in range(2, 8):
                    s = wload(l, TI_UP + i4)
                    for ii in range(4):
                        i = i4 * 4 + ii
                        b = proj_st(s, ii, gt)
                        relu2(i, b)
                if gi + 1 < len(p2groups):
                    prefetch_h(*p2groups[gi + 1])
                for j in range(8):
                    s = wload(l, TI_DN + j)
                    b = nbank()
                    for i in range(32):
                        MM(PS[:, b, 0:gt], W[:, s, i * 128:(i + 1) * 128], HID[:, i, 0:gt], i == 0, i == 31,
                           [("W", s), ("HID", i)], [("ps", b)])
                    STT(XG[:, j, 0:gt], PS[:, b, 0:gt], MS[:, par, tt_, 40 + j:41 + j], XG[:, j, 0:gt], ALU.mult, ALU.add,
                        [("ps", b), ("MS", par), ("XG", j)], [("XG", j)])
                ln2_stats(gt)

                def fin(t0=t0, gt=gt):
                    A_, B_, kA, kB = ln2_rowmath(gt)
                    for j in range(8):
                        t1 = TT_[:, 4 + j % 2, 0:gt]
                        k1 = ("T", 4 + j % 2)
                        TTo("pool", t1, XG[:, j, 0:gt], A_, ALU.mult, [("XG", j), kA], [k1])
                        TTo("dve", t1, t1, B_, ALU.add, [k1, kB], [k1])
                        xo = TT_[:, 6 + j % 2, 0:gt]
                        ko = ("T", 6 + j % 2)
                        TS("dve", xo, t1, sp(64 + j), sp(72 + j), ALU.mult, ALU.add, [k1, "SMALLP"], [ko])
                        if last:
                            dst = outT[j * 128:(j + 1) * 128, t0 - CTX:t0 - CTX + gt]
                        else:
                            dst = xs[j * 128:(j + 1) * 128, t0:t0 + gt]
                        DMA("pool", dst, xo, [ko], [("x", t0, j)], "xo%d" % (j % 2))

                if gi + 1 < len(p2groups):
                    deferred[0] = fin
                else:
                    fin()

            if l + 1 < depth:
                mod_derive(l + 1)

        if dbg:
            xdbg = nc.dram_tensor("xdbg", [D, T], F32, kind="ExternalOutput").ap()
            DMA("sp", xdbg[:, :], xs[:, :], [k for k in list(S.lastw.keys()) if isinstance(k, tuple) and k[0] == "x"],
                [("x", "dbg", 0)], "dbg")
        S.add("sp", None, [k for k in list(S.lastw.keys()) if isinstance(k, tuple) and k[0] == "x"], ["final"])
        with nc.Block() as block:
            nsem = S.emit(nc, stack, block)
        print("program: %d ops, %d semaphores" % (len(S.ops), nsem))
    return nc


def _st(Wm):
    K, N = Wm.shape
    kc, oc = K // 128, N // 128
    return np.ascontiguousarray(Wm.reshape(kc, 128, oc, 128).transpose(1, 2, 0, 3)).reshape(128, oc * kc * 128)


def _mv(Wm):
    K, N = Wm.shape
    kc = K // 128
    return np.ascontiguousarray(Wm.reshape(kc, 128, N).transpose(1, 0, 2)).reshape(128, kc * N)


def _pp(v):
    return np.ascontiguousarray(v.reshape(-1, 128).T)


def _pack_layer(w_in, w_branch, w_out, w_up, w_down):
    swap = np.arange(512).reshape(-1, 2)[:, ::-1].reshape(-1)
    wq = w_in[:, OFF_Q:OFF_K]
    wk = w_in[:, OFF_K:OFF_V]
    wv = w_in[:, OFF_V:OFF_SGU]
    wu = w_in[:, OFF_SGU:OFF_SGU + 512]
    wsv = w_in[:, OFF_SGU + 512:OFF_SCONV]
    wgb = w_in[:, OFF_SCONV:OFF_SCONV + 512]
    wgc = w_in[:, OFF_SCONV + 512:OFF_SCONV + 1024]
    wxt = w_in[:, OFF_SCONV + 1024:OFF_CCONV]
    wa = w_in[:, OFF_CCONV:OFF_CCONV + 512]
    wb = w_in[:, OFF_CCONV + 512:OFF_GATE]
    wg = w_in[:, OFF_GATE:]

    def inter(a, b):
        return np.concatenate([x for c in range(4) for x in (a[:, c * 128:(c + 1) * 128], b[:, c * 128:(c + 1) * 128])], axis=1)

    gate_cols = np.concatenate([wg[:, k * 1024 + j * 128: k * 1024 + (j + 1) * 128] for j in range(8) for k in range(4)], axis=1)
    parts = [
        _st(wk), _st(wk[:, swap]), _mv(wv), _st(inter(wgc, wxt)), _st(inter(wa, wb)),
        _st(wq), _st(wq[:, swap]), _st(wu), _mv(wsv), _st(wgb), _st(gate_cols),
    ]
    br = []
    for j in range(8):
        blk = np.stack([w_branch[k, kc * 128:(kc + 1) * 128, j * 128:(j + 1) * 128] for k in range(4) for kc in range(4)], axis=1)
        br.append(blk.reshape(128, 16 * 128))
    parts.append(np.concatenate(br, axis=1))
    parts += [_st(w_out), _st(w_up), _st(w_down)]
    out = np.concatenate(parts, axis=1)
    assert out.shape == (128, NCOLS), out.shape
    return out


def _rope_tables():
    rows = SEQ // GRID_W
    row = np.repeat(np.arange(rows, dtype=np.float32), GRID_W)
    col = np.tile(np.arange(GRID_W, dtype=np.float32), rows)
    inv_freq = (np.float32(ROPE_THETA) ** (-np.arange(16, dtype=np.float32) / np.float32(16))).astype(np.float32)
    ang = np.concatenate([row[:, None] * inv_freq, col[:, None] * inv_freq], axis=-1).astype(np.float32)
    cos, sin = np.cos(ang).astype(np.float32), np.sin(ang).astype(np.float32)
    C = np.zeros((128, SEQ), np.float32)
    Sg = np.zeros((128, SEQ), np.float32)
    for p in range(128):
        d = p % 64
        i = d // 2
        C[p] = cos[:, i]
        Sg[p] = -sin[:, i] if d % 2 == 0 else sin[:, i]
    return C, Sg


_CACHE = {}


def kernel(x, c, ctx, c_ctx, w_ada, b_ada, w_in, lam_q1, lam_k1, lam_q2, lam_k2,
           attn_subln_g, sgu_ln_g, sgu_ln_b, sgu_w, sgu_b, sconv_w, cconv_w, cconv_b,
           cconv_ln_g, cconv_ln_b, w_branch, w_out, ln1_g, ln1_b, w_up, w_down, ln2_g, ln2_b):
    f = lambda a: np.asarray(a, dtype=np.float32)
    x, c, ctx, c_ctx = f(x), f(c), f(ctx), f(c_ctx)
    B = x.shape[0]
    wbig = np.stack([_pack_layer(f(w_in[l]), f(w_branch[l]), f(w_out[l]), f(w_up[l]), f(w_down[l])) for l in range(DEPTH)])
    wada = np.stack([np.ascontiguousarray(f(w_ada[l]).reshape(8, 128, 48, 128).transpose(2, 1, 0, 3)).reshape(48, 128, 1024)
                     for l in range(DEPTH)])
    smallp = np.zeros((128, DEPTH * SP_COLS), np.float32)
    bcast = np.zeros((DEPTH, NB), np.float32)
    sguw = np.zeros((DEPTH, 128, 512), np.float32)
    for l in range(DEPTH):
        b0 = l * SP_COLS
        smallp[:, b0:b0 + 48] = _pp(f(b_ada[l]))
        smallp[:, b0 + 48:b0 + 56] = _pp(f(ln1_g[l]))
        smallp[:, b0 + 56:b0 + 64] = _pp(f(ln1_b[l]))
        smallp[:, b0 + 64:b0 + 72] = _pp(f(ln2_g[l]))
        smallp[:, b0 + 72:b0 + 80] = _pp(f(ln2_b[l]))
        sw = f(sconv_w[l])
        cw = f(cconv_w[l])
        for ch in range(4):
            smallp[:, b0 + 80 + ch * 3: b0 + 80 + ch * 3 + 3] = sw[:, ch * 128:(ch + 1) * 128].T
            smallp[:, b0 + 92 + ch * 31: b0 + 92 + ch * 31 + 31] = cw[:, ch * 128:(ch + 1) * 128].T
        smallp[:, b0 + 216:b0 + 220] = _pp(f(cconv_b[l]))
        smallp[:, b0 + 220:b0 + 224] = _pp(f(cconv_ln_g[l]))
        smallp[:, b0 + 224:b0 + 228] = _pp(f(cconv_ln_b[l]))
        bcast[l, 0:512] = f(sgu_ln_g[l])
        bcast[l, 512:1024] = f(sgu_ln_b[l])
        bcast[l, 1024:1536] = f(sgu_b[l]).reshape(-1)
        bcast[l, 1536:1664] = f(attn_subln_g[l])
        bcast[l, 1664:1728] = f(lam_q1[l])
        bcast[l, 1728:1792] = f(lam_k1[l])
        bcast[l, 1792:1856] = f(lam_q2[l])
        bcast[l, 1856:1920] = f(lam_k2[l])
        sguw[l] = f(sgu_w[l]).transpose(2, 0, 1).reshape(128, 512)
    ropeC, ropeS = _rope_tables()
    ident = np.eye(128, dtype=np.float32)
    in_maps = []
    for b in range(B):
        xT = np.ascontiguousarray(np.concatenate([ctx[b], x[b]], axis=0).T)
        cs = np.stack([c[b], c_ctx], axis=1).reshape(8, 128, 2).transpose(1, 0, 2).reshape(128, 16)
        in_maps.append({"xT": xT, "cs": np.ascontiguousarray(cs), "wbig": wbig, "wada": wada, "smallp": smallp,
                        "bcast": bcast, "sguw": sguw, "ropeC": ropeC, "ropeS": ropeS, "ident": ident})
    if "nc" not in _CACHE:
        _CACHE["nc"] = build_program()
    res = run_bass_kernel_spmd(_CACHE["nc"], in_maps, core_ids=list(range(B)))
    out = np.stack([np.ascontiguousarray(r["outT"].T) for r in res.results], axis=0)
    return out.astype(np.float32)
```

```python
import math
from contextlib import ExitStack

import numpy as np
import concourse.bass as bass
import concourse.mybir as mybir
from concourse.bass_utils import run_bass_kernel_spmd

F32 = mybir.dt.float32
BF16 = mybir.dt.bfloat16
AF = mybir.ActivationFunctionType
ALU = mybir.AluOpType

D = 1024
DEPTH = 4
SEQ = 4096
CTX = 256
T = SEQ + CTX
NKT = T // 128
GRID_W = 64
D_FF = 4096
LN_EPS = 1e-5
ALPHA = (2 * DEPTH) ** 0.25
ROPE_THETA = 10000.0
OFF_Q, OFF_K, OFF_V, OFF_SGU, OFF_SCONV, OFF_CCONV, OFF_GATE = 0, 512, 1024, 1536, 2560, 4096, 5120
NTILE = 42
TCOLS = 4096
NCOLS = NTILE * TCOLS
SP_COLS = 228
NB = 1920
TP = 4416
TI_K, TI_KS, TI_V, TI_SC, TI_CC = 0, 1, 2, 3, 5
TI_Q, TI_QS, TI_SU, TI_SV, TI_GB, TI_GATE, TI_BR, TI_WO, TI_UP, TI_DN = 7, 8, 9, 10, 11, 12, 20, 24, 26, 34


class Sched:
    CAP = 28000

    def __init__(self):
        self.ops = []
        self.lastw = {}
        self.readers = {}
        self.wait_all = set()

    R1N = ("YT", "U", "VTOK", "VN", "PW", "GW", "HID")
    R2N = ("QTZ", "PT", "RC", "RS", "ACC", "SQ", "G", "MIXG", "ACCst")

    def add(self, eng, fn, r=(), w=(), lane=None):
        i = len(self.ops)
        r = list(r)
        w = list(w)
        if not any(k in ("R1all", "R2all") for k in w):
            for k in r + w:
                nm = k[0] if isinstance(k, tuple) else k
                if nm in self.R1N and "R1all" not in r:
                    r.append("R1all")
                if nm in self.R2N and "R2all" not in r:
                    r.append("R2all")
        deps = set()
        for k in r:
            x = self.lastw.get(k)
            if x is not None:
                deps.add(x)
        for k in w:
            x = self.lastw.get(k)
            if x is not None:
                deps.add(x)
            rd = self.readers.get(k)
            if rd:
                deps.update(rd.values())
        stream = ("L", lane) if lane else ("E", eng)
        for k in w:
            self.lastw[k] = i
            self.readers[k] = {}
        ws = set(w)
        for k in r:
            if k not in ws:
                self.readers.setdefault(k, {})[stream] = i
        self.ops.append([eng, fn, lane, deps])
        return i

    def emit(self, nc, stack, block):
        ops = self.ops
        n = len(ops)
        stream_of = [("L", o[2]) if o[2] else ("E", o[0]) for o in ops]
        waits = [None] * n
        marked = [False] * n
        waited = {}
        for i, (eng, fn, lane, deps) in enumerate(ops):
            best = {}
            for d in deps:
                s = stream_of[d]
                if s == ("E", "pe") and eng == "pe" and lane is None:
                    continue
                if d > best.get(s, -1):
                    best[s] = d
            wl = []
            we = waited.setdefault(eng, {})
            for s, d in best.items():
                if we.get(s, -1) >= d:
                    continue
                we[s] = d
                wl.append(d)
                marked[d] = True
            waits[i] = wl
        sem_of = [None] * n
        cur = {}
        lane_total = {}

        def newsem(tag):
            return stack.enter_context(nc.semaphore("s_%s_%d" % (tag, len(cur_all))))

        cur_all = []
        for i, (eng, fn, lane, deps) in enumerate(ops):
            if lane:
                key = ("L", lane)
                inc = 16
            elif marked[i]:
                key = ("E", eng)
                inc = 1
            else:
                continue
            st = cur.get(key)
            if st is None or st[1] + inc > self.CAP:
                sem = newsem(key[1])
                cur_all.append(sem)
                st = [sem, 0]
                cur[key] = st
            st[1] += inc
            sem_of[i] = (st[0], st[1], inc)
            if lane:
                lane_total[lane] = (st[0], st[1])
        by_eng = {}
        for i, o in enumerate(ops):
            by_eng.setdefault(o[0], []).append(i)

        def run(e, idxs):
            for i in idxs:
                eng, fn, lane, deps = ops[i]
                for d in waits[i]:
                    dl = ops[d][2]
                    if dl in self.wait_all:
                        sem, val = lane_total[dl]
                    else:
                        sem, val, _ = sem_of[d]
                    e.wait_ge(sem, val)
                if fn is None:
                    continue
                ins = fn(e)
                if sem_of[i] is not None:
                    ins.then_inc(sem_of[i][0], sem_of[i][2])

        deco = {"pe": block.tensor, "act": block.scalar, "dve": block.vector,
                "pool": block.gpsimd, "sp": block.sync}
        for name, idxs in by_eng.items():
            def _f(e, idxs=idxs):
                run(e, idxs)
            deco[name](_f)
        return len(cur_all)


def build_program(depth=DEPTH, dbg=False):
    nc = bass.Bass("TRN2", target_bir_lowering=False)
    S = Sched()
    S.wait_all.add("const")

    def din(name, shape, dt=F32):
        return nc.dram_tensor(name, list(shape), dt, kind="ExternalInput").ap()

    xT_in = din("xT", [D, T])
    cs_in = din("cs", [128, 16])
    wbig = din("wbig", [DEPTH, 128, NCOLS])
    wada = din("wada", [DEPTH, 48, 128, 1024])
    smallp_in = din("smallp", [128, DEPTH * SP_COLS])
    bcast_in = din("bcast", [DEPTH, NB])
    sguw_in = din("sguw", [DEPTH, 128, 512])
    ropeC_in = din("ropeC", [128, SEQ])
    ropeS_in = din("ropeS", [128, SEQ])
    ident_in = din("ident", [128, 128])
    outT = nc.dram_tensor("outT", [D, SEQ], F32, kind="ExternalOutput").ap()
    wbf = nc.dram_tensor("wbf", [DEPTH, 128, NCOLS], BF16, kind="Internal").ap()
    xs = nc.dram_tensor("xs", [D, T], F32, kind="Internal").ap()
    cvn = nc.dram_tensor("cvn", [2, 4, 128, TP], F32, kind="Internal").ap()

    with ExitStack() as stack:
        def sb(name, shape, dt=F32):
            return stack.enter_context(nc.sbuf_tensor(name, list(shape), dt))

        KT = sb("KT", [128, 4, T], BF16)
        VA = sb("VA", [128, NKT, 4, 130], BF16)
        W = sb("W", [128, 3, TCOLS], BF16)
        XG = sb("XG", [128, 8, 512])
        H = sb("H", [128, 8, 512], BF16)
        R1 = sb("R1", [128, 8704])
        R2 = sb("R2", [128, 4096])
        TT_ = sb("TT", [128, 8, 512])
        SMALLP = sb("SMALLP", [128, DEPTH * SP_COLS])
        BC = sb("BC", [128, NB])
        SGUW = sb("SGUW", [128, 512], BF16)
        MODT = sb("MODT", [128, 48, 2])
        MS = sb("MS", [128, 2, 2, 48])
        CSL = sb("CSL", [128, 16])
        SC = sb("SC", [128, 8, 2])
        IDENT = sb("IDENT", [128, 128])
        ONES = sb("ONES", [128, 128])
        GSUB = sb("GSUB", [128, 128])
        OT = sb("OT", [128, 3, 128])
        SMALL = sb("SMALL", [128, 128])
        OACC = sb("OACC", [128, 8, 130])
        YB = sb("YB", [128, 4, 128])
        EPS = sb("EPS", [128, 2])
        ZERO = sb("ZERO", [128, 8, 16])
        ZB = sb("ZB", [128, 512], BF16)
        PS = stack.enter_context(nc.psum_tensor("PS", [128, 8, 512], F32))

        R1b = R1.bitcast(BF16)
        R2b = R2.bitcast(BF16)
        YT = R1b[:, 0:8192].rearrange("p (c t) -> p c t", c=16)
        HID = R1b[:, 0:16384].rearrange("p (c t) -> p c t", c=32)
        PW = R1[:, 4096:6272].rearrange("p (c t) -> p c t", c=4)
        GW = R1[:, 6272:8448].rearrange("p (c t) -> p c t", c=4)
        U = R1b[:, 8192:10240].rearrange("p (c t) -> p c t", c=4)
        VTOK = R1[:, 5120:7168].rearrange("p (s t) -> p s t", s=4)
        VN = R1b[:, 14336:16384].rearrange("p (s t) -> p s t", s=4)
        QTZ = R2b[:, 0:4096].rearrange("p (z c t) -> p z c t", z=2, c=4)
        PT = R2b[:, 4096:6144].rearrange("p (s t) -> p s t", s=4)
        RC = R2[:, 3072:3584]
        RS = R2[:, 3584:4096]
        ACC = R2[:, 0:2048].rearrange("p (c t) -> p c t", c=4)
        SQ = R2[:, 2048:4096].rearrange("p (c t) -> p c t", c=4)
        GATES = R2[:, 0:2048].rearrange("p (c t) -> p c t", c=4)
        MIXG = R2b[:, 4096:8192].rearrange("p (c t) -> p c t", c=8)

        def MM(out, lhsT, rhs, start, stop, r, w):
            S.add("pe", lambda e: e.matmul(out, lhsT, rhs, start=start, stop=stop), r, w)

        def ACT(out, in_, func, r, w, bias=None, scale=None, accum=None):
            kw = {}
            if bias is not None:
                kw["bias"] = bias
            if scale is not None:
                kw["scale"] = scale
            if accum is not None:
                kw["accum_out"] = accum
            S.add("act", lambda e: e.activation(out, in_, func, **kw), r, w)

        def TS(eng, out, in0, s1, s2, op0, op1, r, w):
            if op1 is None:
                S.add(eng, lambda e: e.tensor_scalar(out, in0, s1, None, op0), r, w)
            else:
                S.add(eng, lambda e: e.tensor_scalar(out, in0, s1, s2, op0, op1), r, w)

        def TTo(eng, out, in0, in1, op, r, w):
            S.add(eng, lambda e: e.tensor_tensor(out, in0, in1, op), r, w)

        def STT(out, in0, scalar, in1, op0, op1, r, w):
            S.add("dve", lambda e: e.scalar_tensor_tensor(out, in0, scalar, in1, op0, op1), r, w)

        def CP(eng, out, in_, r, w):
            if eng == "act":
                S.add("act", lambda e: e.copy(out, in_), r, w)
            else:
                S.add(eng, lambda e: e.tensor_copy(out, in_), r, w)

        def DMA(eng, out, in_, r, w, lane):
            S.add(eng, lambda e: e.dma_start(out=out, in_=in_), r, w, lane=lane)

        def MEMSET(eng, ap, val, w):
            S.add(eng, lambda e: e.memset(ap, val), (), w)

        def fence(keys):
            S.add("pool", lambda e: e.memset(SMALL[0:1, 63:64], 0.0), (), list(keys) + ["fdummy"])

        bank_ctr = [0]

        def nbank():
            b = bank_ctr[0] % 8
            bank_ctr[0] += 1
            return b

        wslot_ctr = [0]

        def piece_of(ti):
            return ti // 2

        def wload(l, ti):
            s = wslot_ctr[0] % 3
            wslot_ctr[0] += 1
            DMA("sp", W[:, s, :], wbf[l, :, ti * TCOLS:(ti + 1) * TCOLS],
                [("wbf", l, piece_of(ti))], [("W", s)], "w%d" % s)
            return s

        def cast_piece(l, p):
            DMA("pool", wbf[l, :, p * 2 * TCOLS:(p + 1) * 2 * TCOLS],
                wbig[l, :, p * 2 * TCOLS:(p + 1) * 2 * TCOLS],
                [], [("wbf", l, p)], "cast%d_%d" % (p, l % 2))

        def cast_layer(l):
            for p in range(NTILE // 2):
                cast_piece(l, p)

        DMA("sp", SMALLP[:, :], smallp_in[:, :], [], ["SMALLP"], "const")
        DMA("sp", CSL[:, :], cs_in[:, :], [], ["CSL"], "const")
        DMA("sp", IDENT[:, :], ident_in[:, :], [], ["IDENT"], "const")
        MEMSET("dve", ONES[:, :], 1.0, ["ONES"])
        MEMSET("dve", EPS[:, 0:1], LN_EPS, ["EPS"])
        MEMSET("dve", EPS[:, 1:2], LN_EPS / (ALPHA * ALPHA), ["EPS"])
        MEMSET("dve", ZERO[:, :, :], 0.0, ["ZERO"])
        MEMSET("dve", ZB[:, :], 0.0, ["ZB"])
        MEMSET("pool", VA[:, :, :, 128:130], 0.0, ["VA"])
        MEMSET("pool", VA[:, :, :, 128:129], 1.0, ["VA"])
        for a in (0, 272, 288, 4400):
            DMA("sp", cvn[:, :, :, a:a + 16].rearrange("s c p t -> p (s c) t"), ZERO[:, :, :],
                ["ZERO"], [("cvnpad", a)], "const")
        ACT(SC[:, :, :], CSL[:, :].rearrange("p (c t) -> p c t", t=2), AF.Silu, ["CSL"], ["SC"])
        cast_layer(0)

        groups = [(0, CTX, True)] + [(CTX + 512 * g, 512, False) for g in range(SEQ // 512)]

        def mod_piece(lm, jj, use_w):
            if use_w:
                s_ = wslot_ctr[0] % 3
                wslot_ctr[0] += 1
                wa = W[:, s_, :].bitcast(F32)
                wkeys = [("W", s_)]
                lane = "w%d" % s_
            else:
                slot = jj % 2
                wa = TT_[:, slot * 4:(slot + 1) * 4, :].rearrange("p a b -> p (a b)")
                wkeys = [("T", slot * 4 + i) for i in range(4)]
                lane = "wa%d" % slot
            DMA("sp", wa.rearrange("p (j c) -> p j c", j=2),
                wada[lm, 2 * jj:2 * jj + 2, :, :].rearrange("j p c -> p j c"), [], wkeys, lane)
            mb = nbank()
            for j2 in range(2):
                for kc in range(8):
                    MM(PS[:, mb, 2 * j2:2 * j2 + 2], wa[:, j2 * 1024 + kc * 128: j2 * 1024 + (kc + 1) * 128],
                       SC[:, kc, :], kc == 0, kc == 7, wkeys + ["SC"], [("ps", mb)])
            for t in range(2):
                TTo("dve", MODT[:, 2 * jj:2 * jj + 2, t], PS[:, mb, 0:4].rearrange("p (j t) -> p j t", t=2)[:, :, t],
                    SMALLP[:, lm * SP_COLS + 2 * jj: lm * SP_COLS + 2 * jj + 2], ALU.add,
                    [("ps", mb), "SMALLP"], [("MODT", jj, t)])

        def mod_derive(lm):
            par = lm % 2
            b0 = lm * SP_COLS
            mk = [("MODT", jj, t) for jj in range(24) for t in range(2)]
            for t in range(2):
                m = MODT[:, :, t]
                ms = MS[:, par, t, :]
                kms = [("MS", par)]
                CP("dve", ms[:, 0:8], m[:, 0:8], mk, kms)
                TS("dve", ms[:, 8:16], m[:, 8:16], 1.0, None, ALU.add, None, mk, kms)
                TS("dve", ms[:, 16:24], m[:, 16:24], 1.0 / ALPHA, None, ALU.mult, None, mk, kms)
                TS("dve", SMALL[:, 8:16], m[:, 32:40], 1.0, None, ALU.add, None, mk, ["SM_sc2"])
                TTo("dve", ms[:, 24:32], SMALLP[:, b0 + 48:b0 + 56], SMALL[:, 8:16], ALU.mult, ["SM_sc2", "SMALLP"], kms)
                TTo("dve", SMALL[:, 16:24], SMALLP[:, b0 + 56:b0 + 64], SMALL[:, 8:16], ALU.mult, ["SM_sc2", "SMALLP"], ["SM_b2"])
                TTo("dve", ms[:, 32:40], SMALL[:, 16:24], m[:, 24:32], ALU.add, ["SM_b2"] + mk, kms)
                TS("dve", ms[:, 40:48], m[:, 40:48], 1.0 / ALPHA, None, ALU.mult, None, mk, kms)

        def cvcol(t0):
            return t0 + 16 if t0 < CTX else t0 + 48

        def xkeys(t0, gt):
            return [("x", t0)]

        for l in range(depth):
            last = (l == DEPTH - 1)
            lam_init = 0.8 - 0.6 * math.exp(-0.3 * l)
            spb = l * SP_COLS
            xsrc = xT_in if l == 0 else xs

            def sp(c0, n=1, spb=spb):
                return SMALLP[:, spb + c0: spb + c0 + n]

            DMA("sp", BC[:, :], bcast_in[l:l + 1, :].partition_broadcast(128), [], ["BC"], "bc")
            DMA("sp", TT_[:, 0, :], sguw_in[l, :, :], [], [("T", 0)], "sguw")
            CP("dve", SGUW[:, :], TT_[:, 0, :], [("T", 0)], ["SGUW"])
            TS("dve", GSUB[:, :], BC[:, 1536:1664], 1.0 - lam_init, None, ALU.mult, None, ["BC"], ["GSUB"])
            TTo("dve", OT[:, 0, 0:64], BC[:, 1664:1728], BC[:, 1728:1792], ALU.mult, ["BC"], ["OT0"])
            TTo("dve", OT[:, 0, 64:128], BC[:, 1792:1856], BC[:, 1856:1920], ALU.mult, ["BC"], ["OT0"])
            S.add("dve", lambda e: e.tensor_reduce(SMALL[:, 0:2], OT[:, 0, :].rearrange("p (a b) -> p a b", a=2),
                                                   mybir.AxisListType.X, ALU.add), ["OT0"], ["SM_lam"])
            ACT(SMALL[:, 2:4], SMALL[:, 0:2], AF.Exp, ["SM_lam"], ["SM_lam2"])
            TTo("dve", SMALL[:, 4:5], SMALL[:, 2:3], SMALL[:, 3:4], ALU.subtract, ["SM_lam2"], ["SM_lam3"])
            TS("dve", SMALL[:, 5:6], SMALL[:, 4:5], -1.0, -lam_init, ALU.mult, ALU.add, ["SM_lam3"], ["NEGLAM"])
            NEGLAM = SMALL[:, 5:6]

            par = l % 2
            if l == 0:
                for jj in range(24):
                    mod_piece(0, jj, False)
                mod_derive(0)

            def load_h(t0, gt, isctx):
                tt = 1 if isctx else 0
                DMA("sp", XG[:, :, 0:gt], xsrc.rearrange("(c p) t -> p c t", p=128)[:, :, t0:t0 + gt],
                    ([("x", t0, j) for j in range(8)] if l > 0 else []), [("XG", j) for j in range(8)], "xg")
                for kc in range(8):
                    ACT(H[:, kc, 0:gt], XG[:, kc, 0:gt], AF.Identity, [("XG", kc), ("MS", par)], [("H", kc)],
                        bias=MS[:, par, tt, kc:kc + 1], scale=MS[:, par, tt, 8 + kc:9 + kc])

            def proj_st(slot, ocl, gt, kcn=8, src=None, srck=None):
                b = nbank()
                for kc in range(kcn):
                    rhs = H[:, kc, 0:gt] if src is None else src(kc)
                    MM(PS[:, b, 0:gt], W[:, slot, (ocl * kcn + kc) * 128:(ocl * kcn + kc + 1) * 128], rhs,
                       kc == 0, kc == kcn - 1,
                       [("W", slot), (("H", kc) if srck is None else srck(kc))], [("ps", b)])
                return b

            def rope_to(dst, dstkeys, bq, bs, gt, slot):
                t1 = TT_[:, 4 + (slot % 2) * 2, 0:gt]
                t2 = TT_[:, 5 + (slot % 2) * 2, 0:gt]
                k1 = ("T", 4 + (slot % 2) * 2)
                k2 = ("T", 5 + (slot % 2) * 2)
                TTo("dve", t1, PS[:, bq, 0:gt], RC[:, 0:gt], ALU.mult, [("ps", bq), "RC"], [k1])
                TTo("dve", t2, PS[:, bs, 0:gt], RS[:, 0:gt], ALU.mult, [("ps", bs), "RS"], [k2])
                TTo("pool" if slot % 2 == 0 else "dve", dst, t1, t2, ALU.add, [k1, k2], dstkeys)

            def load_rope(t0, gt):
                DMA("sp", RC[:, 0:gt], ropeC_in[:, t0 - CTX:t0 - CTX + gt], [], ["RC"], "ropeC")
                DMA("sp", RS[:, 0:gt], ropeS_in[:, t0 - CTX:t0 - CTX + gt], [], ["RS"], "ropeS")

            fence(["R1all", "R2all"])
            for (t0, gt, isctx) in groups:
                nqb = gt // 128
                load_h(t0, gt, isctx)
                if not isctx:
                    load_rope(t0, gt)
                sk = wload(l, TI_K)
                if not isctx:
                    sks = wload(l, TI_KS)
                for c in range(4):
                    bq = proj_st(sk, c, gt)
                    if isctx:
                        CP("act", KT[:, c, t0:t0 + gt], PS[:, bq, 0:gt], [("ps", bq)], [("KT", c, t0)])
                    else:
                        bs = proj_st(sks, c, gt)
                        rope_to(KT[:, c, t0:t0 + gt], [("KT", c, t0)], bq, bs, gt, c)
                sv = wload(l, TI_V)
                for tt in range(nqb):
                    b = nbank()
                    for kc in range(8):
                        MM(PS[:, b, :], H[:, kc, tt * 128:(tt + 1) * 128], W[:, sv, kc * 512:(kc + 1) * 512],
                           kc == 0, kc == 7, [("W", sv), ("H", kc)], [("ps", b)])
                    kt = t0 // 128 + tt
                    CP("act", VA[:, kt, :, 0:128], PS[:, b, :].rearrange("p (h e) -> p h e", h=4),
                       [("ps", b)], [("VA", kt)])
                if (not isctx) or (not last):
                    for half in range(2):
                        s = wload(l, TI_SC + half)
                        for cc in range(2):
                            c = half * 2 + cc
                            bg = proj_st(s, cc * 2, gt)
                            bx = proj_st(s, cc * 2 + 1, gt)
                            CP("act", TT_[:, 4 + cc, 0:gt], PS[:, bg, 0:gt], [("ps", bg)], [("T", 4 + cc)])
                            TTo("dve", ACC[:, c, 0:gt], PS[:, bx, 0:gt], TT_[:, 4 + cc, 0:gt], ALU.mult,
                                [("ps", bx), ("T", 4 + cc)], ["ACCst"])
                    DMA("pool", cvn[0, :, :, cvcol(t0):cvcol(t0) + gt].rearrange("c p t -> p c t"), ACC[:, :, 0:gt],
                        ["ACCst"], [("cv", 0, t0)], "pst")
                    for half in range(2):
                        s = wload(l, TI_CC + half)
                        for cc in range(2):
                            c = half * 2 + cc
                            ba = proj_st(s, cc * 2, gt)
                            bb = proj_st(s, cc * 2 + 1, gt)
                            ACT(TT_[:, 6 + cc, 0:gt], PS[:, bb, 0:gt], AF.Sigmoid, [("ps", bb)], [("T", 6 + cc)])
                            TTo("dve", TT_[:, c, 0:gt], PS[:, ba, 0:gt], TT_[:, 6 + cc, 0:gt], ALU.mult,
                                [("ps", ba), ("T", 6 + cc)], ["GLUst"] + [("T", c)])
                    DMA("pool", cvn[1, :, :, cvcol(t0):cvcol(t0) + gt].rearrange("c p t -> p c t"), TT_[:, 0:4, 0:gt],
                        ["GLUst"] + [("T", c) for c in range(4)], [("cv", 1, t0)], "gst")

            def ln_stats(vsrc, vkeys, nch, gt, nfeat, epscol, sqslot_base):
                b1 = nbank()
                b2 = nbank()
                for j in range(nch):
                    sq = TT_[:, sqslot_base + (j % 2), 0:gt]
                    kq = ("T", sqslot_base + (j % 2))
                    ACT(sq, vsrc(j), AF.Square, [vkeys(j)], [kq])
                    MM(PS[:, b1, 0:gt], ONES[:, :], vsrc(j), j == 0, j == nch - 1, ["ONES", vkeys(j)], [("ps", b1)])
                    MM(PS[:, b2, 0:gt], ONES[:, :], sq, j == 0, j == nch - 1, ["ONES", kq], [("ps", b2)])
                ta = TT_[:, 2, 0:gt]
                tb = TT_[:, 3, 0:gt]
                ka, kb = ("T", 2), ("T", 3)
                ACT(ta, PS[:, b1, 0:gt], AF.Copy, [("ps", b1)], [ka], scale=1.0 / nfeat)
                TTo("dve", tb, ta, ta, ALU.mult, [ka], [kb])
                STT(tb, PS[:, b2, 0:gt], 1.0 / nfeat, tb, ALU.mult, ALU.subtract, [("ps", b2), kb], [kb])
                ACT(tb, tb, AF.Sqrt, [kb, "EPS"], [kb], bias=EPS[:, epscol:epscol + 1])
                S.add("dve", lambda e: e.reciprocal(tb, tb), [kb], [kb])
                STT(ta, ta, -1.0, tb, ALU.mult, ALU.mult, [ka, kb], [ka])
                return tb, ta, kb, ka

            def ln2_stats(gt):
                b1 = nbank()
                b2 = nbank()
                for j in range(8):
                    sq = TT_[:, j % 2, 0:gt]
                    kq = ("T", j % 2)
                    ACT(sq, XG[:, j, 0:gt], AF.Square, [("XG", j)], [kq])
                    MM(PS[:, b1, 0:gt], ONES[:, :], XG[:, j, 0:gt], j == 0, j == 7, ["ONES", ("XG", j)], [("ps", b1)])
                    MM(PS[:, b2, 0:gt], ONES[:, :], sq, j == 0, j == 7, ["ONES", kq], [("ps", b2)])
                TS("dve", TT_[:, 2, 0:gt], PS[:, b1, 0:gt], 1.0 / 1024.0, None, ALU.mult, None, [("ps", b1)], [("T", 2)])
                TS("dve", TT_[:, 3, 0:gt], PS[:, b2, 0:gt], 1.0 / 1024.0, None, ALU.mult, None, [("ps", b2)], [("T", 3)])

            def ln2_rowmath(gt):
                ta = TT_[:, 2, 0:gt]
                tb = TT_[:, 3, 0:gt]
                tc = TT_[:, 0, 0:gt]
                ka, kb, kc_ = ("T", 2), ("T", 3), ("T", 0)
                TTo("dve", tc, ta, ta, ALU.mult, [ka], [kc_])
                TTo("dve", tb, tb, tc, ALU.subtract, [kb, kc_], [kb])
                ACT(tb, tb, AF.Sqrt, [kb, "EPS"], [kb], bias=EPS[:, 1:2])
                S.add("dve", lambda e: e.reciprocal(tb, tb), [kb], [kb])
                STT(ta, ta, -1.0, tb, ALU.mult, ALU.mult, [ka, kb], [ka])
                return tb, ta, kb, ka

            def prefetch_h(t0n, gtn, isctxn):
                ttn = 1 if isctxn else 0
                for kc in range(8):
                    slot = kc % 2
                    DMA("pool", TT_[:, slot, 0:gtn], xsrc[kc * 128:(kc + 1) * 128, t0n:t0n + gtn],
                        ([("x", t0n, kc)] if l > 0 else []), [("T", slot)], "xp%d" % slot)
                    ACT(H[:, kc, 0:gtn], TT_[:, slot, 0:gtn], AF.Identity, [("T", slot), ("MS", par)], [("H", kc)],
                        bias=MS[:, par, ttn, kc:kc + 1], scale=MS[:, par, ttn, 8 + kc:9 + kc])

            p2groups = [g_ for g_ in groups if not (g_[2] and last)]
            deferred = [None]
            for gi, (t0, gt, isctx) in enumerate(p2groups):
                nqb = gt // 128
                tt_ = 1 if isctx else 0
                fence(["R1all", "R2all"])
                if gi == 0:
                    load_h(t0, gt, isctx)
                sq_ = wload(l, TI_Q)
                if not isctx:
                    sqs = wload(l, TI_QS)
                w0 = cvcol(t0) - 16
                nbr = [("cv", 0, tn) for (tn, gn, cn) in groups if cn == isctx and abs(tn - t0) <= 512]
                DMA("sp", PW[:, :, 0:gt + 32], cvn[0, :, :, w0:w0 + gt + 32].rearrange("c p t -> p c t"),
                    nbr + [("cvnpad", a) for a in (0, 272, 288, 4400)], ["PW"], "pw")
                nbr = [("cv", 1, tn) for (tn, gn, cn) in groups if cn == isctx and abs(tn - t0) <= 512]
                DMA("sp", GW[:, :, 0:gt + 32], cvn[1, :, :, w0:w0 + gt + 32].rearrange("c p t -> p c t"),
                    nbr + [("cvnpad", a) for a in (0, 272, 288, 4400)], ["GW"], "gw")
                if not isctx:
                    load_rope(t0, gt)
                MEMSET("pool", QTZ[0:64, 1, :, 0:gt], 0.0, ["QTZ"])
                MEMSET("dve", QTZ[64:128, 0, :, 0:gt], 0.0, ["QTZ"])
                for c in range(4):
                    bq = proj_st(sq_, c, gt)
                    if isctx:
                        CP("act", QTZ[0:64, 0, c, 0:gt], PS[0:64, bq, 0:gt], [("ps", bq)], ["QTZ"])
                        CP("act", QTZ[64:128, 1, c, 0:gt], PS[64:128, bq, 0:gt], [("ps", bq)], ["QTZ"])
                    else:
                        bs = proj_st(sqs, c, gt)
                        t1 = TT_[:, 4 + (c % 2) * 2, 0:gt]
                        t2 = TT_[:, 5 + (c % 2) * 2, 0:gt]
                        k1 = ("T", 4 + (c % 2) * 2)
                        k2 = ("T", 5 + (c % 2) * 2)
                        TTo("dve", t1, PS[:, bq, 0:gt], RC[:, 0:gt], ALU.mult, [("ps", bq), "RC"], [k1])
                        TTo("dve", t2, PS[:, bs, 0:gt], RS[:, 0:gt], ALU.mult, [("ps", bs), "RS"], [k2])
                        TTo("pool", QTZ[0:64, 0, c, 0:gt], t1[0:64, :], t2[0:64, :], ALU.add, [k1, k2],
                            ["QTZ"])
                        TTo("dve", QTZ[64:128, 1, c, 0:gt], t1[64:128, :], t2[64:128, :], ALU.add, [k1, k2],
                            ["QTZ"])
                if deferred[0] is not None:
                    deferred[0]()
                    deferred[0] = None
                if l + 1 < depth and not isctx:
                    g_ = (t0 - CTX) // 512
                    for p_ in range(3 * g_, min(3 * g_ + 3, NTILE // 2)):
                        cast_piece(l + 1, p_)
                kts = list(range(0, CTX // 128)) if isctx else list(range(NKT))
                sctr = 0
                tctr = [0]
                pending = [None]

                def do_transposes(hh):
                    for qb in range(nqb):
                        tb_ = 4 + tctr[0] % 4
                        tctr[0] += 1
                        S.add("pe", lambda e, tb_=tb_, qb=qb: e.transpose(PS[:, tb_, 0:128], YB[:, qb, :], IDENT[:, :]),
                              [("YB", qb), "IDENT"], [("ps", tb_)])
                        CP("act", YT[:, 4 + hh, qb * 128:(qb + 1) * 128], PS[:, tb_, 0:128], [("ps", tb_)],
                           [("YT", 4 + hh)])
                for h in range(4):
                    hp = h % 2
                    cs_ = [h // 2, 2 + h // 2]
                    kx = ("XG", 2 * h)
                    TS("dve", XG[:, 2 * h, 0:gt], GW[:, h, 1:1 + gt], sp(92 + h * 31), sp(216 + h), ALU.mult, ALU.add,
                       ["GW", "SMALLP", kx], [kx])
                    for tap in range(1, 31):
                        STT(XG[:, 2 * h, 0:gt], GW[:, h, 1 + tap:1 + tap + gt], sp(92 + h * 31 + tap), XG[:, 2 * h, 0:gt],
                            ALU.mult, ALU.add, ["GW", "SMALLP", kx], [kx])
                    for c_ in (range(4) if h == 3 else ()):
                        ky = ("XG", 2 * c_ + 1)
                        TS("dve", XG[:, 2 * c_ + 1, 0:gt], PW[:, c_, 15:15 + gt], sp(80 + c_ * 3), None, ALU.mult, None,
                           ["PW", "SMALLP", ky], [ky])
                        for tap in (1, 2):
                            STT(XG[:, 2 * c_ + 1, 0:gt], PW[:, c_, 15 + tap:15 + tap + gt], sp(80 + c_ * 3 + tap),
                                XG[:, 2 * c_ + 1, 0:gt], ALU.mult, ALU.add, ["PW", "SMALLP", ky], [ky])

                    def accap(a, ncol=129):
                        return PS[:, a // 2, (a % 2) * 256:(a % 2) * 256 + ncol]

                    def qk(kt, sidx):
                        for m in range(2):
                            sb_ = 4 + (sidx + m) % 4
                            MM(PS[:, sb_, 0:gt], KT[:, cs_[m], kt * 128:(kt + 1) * 128], QTZ[:, hp, cs_[m], 0:gt],
                               True, True, [("KT", cs_[m], (0 if kt * 128 < CTX else CTX + ((kt * 128 - CTX) // 512) * 512)), "QTZ"],
                               [("ps", sb_)])

                    def ex(kt, sidx):
                        for m in range(2):
                            sb_ = 4 + (sidx + m) % 4
                            pslot = (sidx + m) % 4
                            ACT(PT[:, pslot, 0:gt], PS[:, sb_, 0:gt], AF.Exp, [("ps", sb_)], [("PT", pslot)],
                                scale=0.125)

                    def av(kt, sidx, first, lastk):
                        for m in range(2):
                            pslot = (sidx + m) % 4
                            for qb in range(nqb):
                                a = m * nqb + qb
                                MM(accap(a), PT[:, pslot, qb * 128:(qb + 1) * 128], VA[:, kt, h, 0:129],
                                   False, lastk and a % 2 == 1, [("PT", pslot), ("VA", kt), "VA"], [("ps", a // 2)])

                    sid = {}
                    for i, kt in enumerate(kts):
                        sid[kt] = sctr
                        sctr += 2
                    for ab in range(nqb):
                        MM(PS[:, ab, :], ZB[:, 0:128], ZB[:, :], True, False, ["ZB"], [("ps", ab)])
                    qk(kts[0], sid[kts[0]])
                    ex(kts[0], sid[kts[0]])
                    for i, kt in enumerate(kts):
                        if i + 1 < len(kts):
                            qk(kts[i + 1], sid[kts[i + 1]])
                            ex(kts[i + 1], sid[kts[i + 1]])
                        av(kt, sid[kt], i == 0, i == len(kts) - 1)
                        if i == min(6, len(kts) - 1) and pending[0] is not None:
                            pending[0]()
                            pending[0] = None
                    if h > 0:
                        do_transposes(h - 1)
                    nacc = 2 * nqb
                    CP("act", OACC[:, 0:4, 0:129],
                       PS[:, 0:2, :].rearrange("p b (o c) -> p (b o) c", o=2)[:, :, 0:129],
                       [("ps", 0), ("ps", 1)], [("OACC", 0)])
                    if nqb == 4:
                        CP("act", OACC[:, 4:8, 0:129],
                           PS[:, 2:4, :].rearrange("p b (o c) -> p (b o) c", o=2)[:, :, 0:129],
                           [("ps", 2), ("ps", 3)], [("OACC", 1)])
                    okeys = [("OACC", 0), ("OACC", 1)] if nqb == 4 else [("OACC", 0)]
                    sums = OACC[:, 0:nacc, 128:129]
                    S.add("dve", lambda e, sums=sums, nacc=nacc: e.reciprocal(
                        SMALL[:, 24:24 + nacc].rearrange("p (a o) -> p a o", o=1), sums), okeys, ["SM_rc"])
                    TS("dve", SMALL[:, 32:32 + nqb], SMALL[:, 24 + nqb:24 + 2 * nqb], NEGLAM, None, ALU.mult, None,
                       ["SM_rc", "NEGLAM"], ["SM_s1"])
                    for qb in range(nqb):
                        a0, a1 = qb, nqb + qb
                        k0, k1_ = ("OACC", a0 // 4), ("OACC", a1 // 4)
                        TS("dve", OT[:, 0, :], OACC[:, a0, 0:128], SMALL[:, 24 + qb:25 + qb], None, ALU.mult, None,
                           [k0, "SM_rc"], ["OT0"])
                        STT(OACC[:, a1, 0:128], OACC[:, a1, 0:128], SMALL[:, 32 + qb:33 + qb], OT[:, 0, :],
                            ALU.mult, ALU.add, [k1_, "SM_s1", "OT0"], [k1_])
                        S.add("dve", lambda e, a1=a1, qb=qb: e.scalar_tensor_tensor(
                            OT[:, 1, :], OACC[:, a1, 0:128], 1.0, OACC[:, a1, 0:128], ALU.mult, ALU.mult,
                            accum_out=SMALL[:, 40 + qb:41 + qb]), [k1_], ["OT1", ("SM_ss", qb)])

                    def part2(nqb=nqb):
                        ACT(SMALL[:, 44:44 + nqb], SMALL[:, 40:40 + nqb], AF.Sqrt,
                            [("SM_ss", q_) for q_ in range(nqb)] + ["EPS"], ["SM_sd"], bias=EPS[:, 0:1], scale=1.0 / 128.0)
                        S.add("dve", lambda e: e.reciprocal(SMALL[:, 48:48 + nqb], SMALL[:, 44:44 + nqb]), ["SM_sd"], ["SM_rr"])
                        for qb in range(nqb):
                            a1 = nqb + qb
                            STT(YB[:, qb, :], OACC[:, a1, 0:128], SMALL[:, 48 + qb:49 + qb], GSUB[:, :], ALU.mult, ALU.mult,
                                [("OACC", a1 // 4), "SM_rr", "GSUB"], [("YB", qb)])
                    pending[0] = part2
                pending[0]()
                pending[0] = None
                do_transposes(3)
                fence(["R1all", "R2all"])
                sv = wload(l, TI_SV)
                for tt in range(nqb):
                    b = nbank()
                    for kc in range(8):
                        MM(PS[:, b, :], H[:, kc, tt * 128:(tt + 1) * 128], W[:, sv, kc * 512:(kc + 1) * 512],
                           kc == 0, kc == 7, [("W", sv), ("H", kc)], [("ps", b)])
                    vs = tt
                    kv = ("VTOK", vs)
                    sm = 64 + tt * 12
                    ks = "SMv%d" % tt
                    ACT(VTOK[:, vs, :], PS[:, b, :], AF.Gelu, [("ps", b)], [kv])
                    S.add("dve", lambda e, vs=vs, sm=sm: e.bn_stats(SMALL[:, sm:sm + 6], VTOK[:, vs, :]), [kv], [ks + "a"])
                    S.add("dve", lambda e, sm=sm: e.bn_aggr(SMALL[:, sm + 6:sm + 8], SMALL[:, sm:sm + 6]), [ks + "a"], [ks + "b"])
                    ACT(SMALL[:, sm + 8:sm + 9], SMALL[:, sm + 7:sm + 8], AF.Sqrt, [ks + "b", "EPS"], [ks + "c"], bias=EPS[:, 0:1])
                    S.add("dve", lambda e, sm=sm: e.reciprocal(SMALL[:, sm + 9:sm + 10], SMALL[:, sm + 8:sm + 9]), [ks + "c"], [ks + "d"])
                    TS("dve", VTOK[:, vs, :], VTOK[:, vs, :], SMALL[:, sm + 6:sm + 7], SMALL[:, sm + 9:sm + 10],
                       ALU.subtract, ALU.mult, [kv, ks + "b", ks + "d"], [kv])
                    TTo("pool", VTOK[:, vs, :], VTOK[:, vs, :], BC[:, 0:512], ALU.mult, [kv, "BC"], [kv])
                    TTo("pool", VN[:, vs, :], VTOK[:, vs, :], BC[:, 512:1024], ALU.add, [kv, "BC"], [("VN", vs)])
                if l + 1 < depth and not isctx:
                    g_ = (t0 - CTX) // 512
                    for jj in range(3 * g_, 3 * g_ + 3):
                        mod_piece(l + 1, jj, True)
                sg = wload(l, TI_GB)
                for c in range(4):
                    b = proj_st(sg, c, gt)
                    TTo("dve", YT[:, 8 + c, 0:gt], PS[:, b, 0:gt], XG[:, 2 * c + 1, 0:gt], ALU.mult,
                        [("ps", b), ("XG", 2 * c + 1)], [("YT", 8 + c)])
                A_, B_, kA, kB = ln_stats(lambda j: XG[:, 2 * j, 0:gt], lambda j: ("XG", 2 * j), 4, gt, 512.0, 0, 0)
                for c in range(4):
                    TTo("pool", SQ[:, c, 0:gt], XG[:, 2 * c, 0:gt], A_, ALU.mult, [("XG", 2 * c), kA], [("SQ", c)])
                    TTo("dve", SQ[:, c, 0:gt], SQ[:, c, 0:gt], B_, ALU.add, [("SQ", c), kB], [("SQ", c)])
                    ACT(YT[:, 12 + c, 0:gt], SQ[:, c, 0:gt], AF.Silu, [("SQ", c), "SMALLP"], [("YT", 12 + c)],
                        bias=sp(224 + c), scale=sp(220 + c))
                DMA("pool", XG[:, :, 0:gt], xsrc.rearrange("(c p) t -> p c t", p=128)[:, :, t0:t0 + gt],
                    ([("x", t0, j) for j in range(8)] if l > 0 else []), [("XG", j) for j in range(8)], "xg2")
                su = wload(l, TI_SU)
                for c in range(4):
                    b = proj_st(su, c, gt)
                    ACT(U[:, c, 0:gt], PS[:, b, 0:gt], AF.Gelu, [("ps", b)], [("U", c)])
                for tt in range(nqb):
                    vs = tt
                    b2 = nbank()
                    for g in range(4):
                        MM(PS[:, b2, g * 128:(g + 1) * 128], VN[:, vs, g * 128:(g + 1) * 128],
                           SGUW[:, g * 128:(g + 1) * 128], True, True, [("VN", vs), "SGUW"], [("ps", b2)])
                    t1 = TT_[:, 4 + tt % 2, :]
                    k1 = ("T", 4 + tt % 2)
                    TTo("dve", t1, PS[:, b2, :], BC[:, 1024:1536], ALU.add, [("ps", b2), "BC"], [k1])
                    TTo("pool", YT[:, 0:4, tt * 128:(tt + 1) * 128], t1.rearrange("p (g t) -> p g t", g=4),
                        U[:, :, tt * 128:(tt + 1) * 128], ALU.mult, [k1] + [("U", c) for c in range(4)],
                        [("YT", c) for c in range(4)])
                fence(["R2all"])
                for j in range(8):
                    sgt = wload(l, TI_GATE + j)
                    if j % 2 == 0:
                        sbr = wload(l, TI_BR + j // 2)
                    for k in range(4):
                        bg = proj_st(sgt, k, gt)
                        ACT(GATES[:, k, 0:gt], PS[:, bg, 0:gt], AF.Sigmoid, [("ps", bg)], [("G", k)])
                    for k in range(4):
                        bp = nbank()
                        for kc in range(4):
                            col = (j % 2) * 2048 + (k * 4 + kc) * 128
                            MM(PS[:, bp, 0:gt], W[:, sbr, col:col + 128], YT[:, k * 4 + kc, 0:gt], kc == 0, kc == 3,
                               [("W", sbr), ("YT", k * 4 + kc)], [("ps", bp)])
                        TTo("dve", TT_[:, 4 + k, 0:gt], PS[:, bp, 0:gt], GATES[:, k, 0:gt], ALU.mult,
                            [("ps", bp), ("G", k)], [("T", 4 + k)])
                    TTo("pool", TT_[:, 4, 0:gt], TT_[:, 4, 0:gt], TT_[:, 5, 0:gt], ALU.add, [("T", 4), ("T", 5)], [("T", 4)])
                    TTo("pool", TT_[:, 6, 0:gt], TT_[:, 6, 0:gt], TT_[:, 7, 0:gt], ALU.add, [("T", 6), ("T", 7)], [("T", 6)])
                    TTo("pool", MIXG[:, j, 0:gt], TT_[:, 4, 0:gt], TT_[:, 6, 0:gt], ALU.add, [("T", 4), ("T", 6)],
                        [("MIXG", j)])
                for half in range(2):
                    so = wload(l, TI_WO + half)
                    for jj in range(4):
                        j = half * 4 + jj
                        b = proj_st(so, jj, gt, 8, lambda kc: MIXG[:, kc, 0:gt], lambda kc: ("MIXG", kc))
                        STT(XG[:, j, 0:gt], PS[:, b, 0:gt], MS[:, par, tt_, 16 + j:17 + j], XG[:, j, 0:gt], ALU.mult, ALU.add,
                            [("ps", b), ("MS", par), ("XG", j)], [("XG", j)])
                A_, B_, kA, kB = ln_stats(lambda j: XG[:, j, 0:gt], lambda j: ("XG", j), 8, gt, 1024.0, 1, 0)
                fence(["R1all"])
                for j in range(8):
                    t1 = TT_[:, 4 + j % 4, 0:gt]
                    k1 = ("T", 4 + j % 4)
                    TTo("pool", t1, XG[:, j, 0:gt], A_, ALU.mult, [("XG", j), kA], [k1])
                    TTo("dve", t1, t1, B_, ALU.add, [k1, kB], [k1])
                    ACT(XG[:, j, 0:gt], t1, AF.Identity, [k1, "SMALLP"], [("XG", j)], bias=sp(56 + j), scale=sp(48 + j))
                    ACT(H[:, j, 0:gt], t1, AF.Identity, [k1, ("MS", par)], [("H", j)],
                        bias=MS[:, par, tt_, 32 + j:33 + j], scale=MS[:, par, tt_, 24 + j:25 + j])
                def relu2(i, b):
                    t1 = TT_[:, 4 + i % 4, 0:gt]
                    k1 = ("T", 4 + i % 4)
                    ACT(t1, PS[:, b, 0:gt], AF.Relu, [("ps", b)], [k1])
                    TTo("pool", HID[:, i, 0:gt], t1, t1, ALU.mult, [k1], [("HID", i)])

                s01 = [wload(l, TI_UP + 0), wload(l, TI_UP + 1)]
                for kc in range(8):
                    for i in range(8):
                        col = ((i % 4) * 8 + kc) * 128
                        MM(PS[:, i, 0:gt], W[:, s01[i // 4], col:col + 128], H[:, kc, 0:gt], kc == 0, kc == 7,
                           [("W", s01[i // 4]), ("H", kc)], [("ps", i)])
                for i in range(8):
                    relu2(i, i)
                for i4 in range(2, 8):
                    s = wload(l, TI_UP + i4)
                    for ii in range(4):
                        i = i4 * 4 + ii
                        b = proj_st(s, ii, gt)
                        relu2(i, b)
                if gi + 1 < len(p2groups):
                    prefetch_h(*p2groups[gi + 1])
                for j in range(8):
                    s = wload(l, TI_DN + j)
                    b = nbank()
                    for i in range(32):
                        MM(PS[:, b, 0:gt], W[:, s, i * 128:(i + 1) * 128], HID[:, i, 0:gt], i == 0, i == 31,
                           [("W", s), ("HID", i)], [("ps", b)])
                    STT(XG[:, j, 0:gt], PS[:, b, 0:gt], MS[:, par, tt_, 40 + j:41 + j], XG[:, j, 0:gt], ALU.mult, ALU.add,
                        [("ps", b), ("MS", par), ("XG", j)], [("XG", j)])
                ln2_stats(gt)

                def fin(t0=t0, gt=gt):
                    A_, B_, kA, kB = ln2_rowmath(gt)
                    for j in range(8):
                        t1 = TT_[:, 4 + j % 2, 0:gt]
                        k1 = ("T", 4 + j % 2)
                        TTo("pool", t1, XG[:, j, 0:gt], A_, ALU.mult, [("XG", j), kA], [k1])
                        TTo("dve", t1, t1, B_, ALU.add, [k1, kB], [k1])
                        xo = TT_[:, 6 + j % 2, 0:gt]
                        ko = ("T", 6 + j % 2)
                        TS("dve", xo, t1, sp(64 + j), sp(72 + j), ALU.mult, ALU.add, [k1, "SMALLP"], [ko])
                        if last:
                            dst = outT[j * 128:(j + 1) * 128, t0 - CTX:t0 - CTX + gt]
                        else:
                            dst = xs[j * 128:(j + 1) * 128, t0:t0 + gt]
                        DMA("pool", dst, xo, [ko], [("x", t0, j)], "xo%d" % (j % 2))

                if gi + 1 < len(p2groups):
                    deferred[0] = fin
                else:
                    fin()

            if l + 1 < depth:
                mod_derive(l + 1)

        if dbg:
            xdbg = nc.dram_tensor("xdbg", [D, T], F32, kind="ExternalOutput").ap()
            DMA("sp", xdbg[:, :], xs[:, :], [k for k in list(S.lastw.keys()) if isinstance(k, tuple) and k[0] == "x"],
                [("x", "dbg", 0)], "dbg")
        S.add("sp", None, [k for k in list(S.lastw.keys()) if isinstance(k, tuple) and k[0] == "x"], ["final"])
        with nc.Block() as block:
            nsem = S.emit(nc, stack, block)
        print("program: %d ops, %d semaphores" % (len(S.ops), nsem))
    return nc


def _st(Wm):
    K, N = Wm.shape
    kc, oc = K // 128, N // 128
    return np.ascontiguousarray(Wm.reshape(kc, 128, oc, 128).transpose(1, 2, 0, 3)).reshape(128, oc * kc * 128)


def _mv(Wm):
    K, N = Wm.shape
    kc = K // 128
    return np.ascontiguousarray(Wm.reshape(kc, 128, N).transpose(1, 0, 2)).reshape(128, kc * N)


def _pp(v):
    return np.ascontiguousarray(v.reshape(-1, 128).T)


def _pack_layer(w_in, w_branch, w_out, w_up, w_down):
    swap = np.arange(512).reshape(-1, 2)[:, ::-1].reshape(-1)
    wq = w_in[:, OFF_Q:OFF_K]
    wk = w_in[:, OFF_K:OFF_V]
    wv = w_in[:, OFF_V:OFF_SGU]
    wu = w_in[:, OFF_SGU:OFF_SGU + 512]
    wsv = w_in[:, OFF_SGU + 512:OFF_SCONV]
    wgb = w_in[:, OFF_SCONV:OFF_SCONV + 512]
    wgc = w_in[:, OFF_SCONV + 512:OFF_SCONV + 1024]
    wxt = w_in[:, OFF_SCONV + 1024:OFF_CCONV]
    wa = w_in[:, OFF_CCONV:OFF_CCONV + 512]
    wb = w_in[:, OFF_CCONV + 512:OFF_GATE]
    wg = w_in[:, OFF_GATE:]

    def inter(a, b):
        return np.concatenate([x for c in range(4) for x in (a[:, c * 128:(c + 1) * 128], b[:, c * 128:(c + 1) * 128])], axis=1)

    gate_cols = np.concatenate([wg[:, k * 1024 + j * 128: k * 1024 + (j + 1) * 128] for j in range(8) for k in range(4)], axis=1)
    parts = [
        _st(wk), _st(wk[:, swap]), _mv(wv), _st(inter(wgc, wxt)), _st(inter(wa, wb)),
        _st(wq), _st(wq[:, swap]), _st(wu), _mv(wsv), _st(wgb), _st(gate_cols),
    ]
    br = []
    for j in range(8):
        blk = np.stack([w_branch[k, kc * 128:(kc + 1) * 128, j * 128:(j + 1) * 128] for k in range(4) for kc in range(4)], axis=1)
        br.append(blk.reshape(128, 16 * 128))
    parts.append(np.concatenate(br, axis=1))
    parts += [_st(w_out), _st(w_up), _st(w_down)]
    out = np.concatenate(parts, axis=1)
    assert out.shape == (128, NCOLS), out.shape
    return out


def _rope_tables():
    rows = SEQ // GRID_W
    row = np.repeat(np.arange(rows, dtype=np.float32), GRID_W)
    col = np.tile(np.arange(GRID_W, dtype=np.float32), rows)
    inv_freq = (np.float32(ROPE_THETA) ** (-np.arange(16, dtype=np.float32) / np.float32(16))).astype(np.float32)
    ang = np.concatenate([row[:, None] * inv_freq, col[:, None] * inv_freq], axis=-1).astype(np.float32)
    cos, sin = np.cos(ang).astype(np.float32), np.sin(ang).astype(np.float32)
    C = np.zeros((128, SEQ), np.float32)
    Sg = np.zeros((128, SEQ), np.float32)
    for p in range(128):
        d = p % 64
        i = d // 2
        C[p] = cos[:, i]
        Sg[p] = -sin[:, i] if d % 2 == 0 else sin[:, i]
    return C, Sg


_CACHE = {}


def kernel(x, c, ctx, c_ctx, w_ada, b_ada, w_in, lam_q1, lam_k1, lam_q2, lam_k2,
           attn_subln_g, sgu_ln_g, sgu_ln_b, sgu_w, sgu_b, sconv_w, cconv_w, cconv_b,
           cconv_ln_g, cconv_ln_b, w_branch, w_out, ln1_g, ln1_b, w_up, w_down, ln2_g, ln2_b):
    f = lambda a: np.asarray(a, dtype=np.float32)
    x, c, ctx, c_ctx = f(x), f(c), f(ctx), f(c_ctx)
    B = x.shape[0]
    wbig = np.stack([_pack_layer(f(w_in[l]), f(w_branch[l]), f(w_out[l]), f(w_up[l]), f(w_down[l])) for l in range(DEPTH)])
    wada = np.stack([np.ascontiguousarray(f(w_ada[l]).reshape(8, 128, 48, 128).transpose(2, 1, 0, 3)).reshape(48, 128, 1024)
                     for l in range(DEPTH)])
    smallp = np.zeros((128, DEPTH * SP_COLS), np.float32)
    bcast = np.zeros((DEPTH, NB), np.float32)
    sguw = np.zeros((DEPTH, 128, 512), np.float32)
    for l in range(DEPTH):
        b0 = l * SP_COLS
        smallp[:, b0:b0 + 48] = _pp(f(b_ada[l]))
        smallp[:, b0 + 48:b0 + 56] = _pp(f(ln1_g[l]))
        smallp[:, b0 + 56:b0 + 64] = _pp(f(ln1_b[l]))
        smallp[:, b0 + 64:b0 + 72] = _pp(f(ln2_g[l]))
        smallp[:, b0 + 72:b0 + 80] = _pp(f(ln2_b[l]))
        sw = f(sconv_w[l])
        cw = f(cconv_w[l])
        for ch in range(4):
            smallp[:, b0 + 80 + ch * 3: b0 + 80 + ch * 3 + 3] = sw[:, ch * 128:(ch + 1) * 128].T
            smallp[:, b0 + 92 + ch * 31: b0 + 92 + ch * 31 + 31] = cw[:, ch * 128:(ch + 1) * 128].T
        smallp[:, b0 + 216:b0 + 220] = _pp(f(cconv_b[l]))
        smallp[:, b0 + 220:b0 + 224] = _pp(f(cconv_ln_g[l]))
        smallp[:, b0 + 224:b0 + 228] = _pp(f(cconv_ln_b[l]))
        bcast[l, 0:512] = f(sgu_ln_g[l])
        bcast[l, 512:1024] = f(sgu_ln_b[l])
        bcast[l, 1024:1536] = f(sgu_b[l]).reshape(-1)
        bcast[l, 1536:1664] = f(attn_subln_g[l])
        bcast[l, 1664:1728] = f(lam_q1[l])
        bcast[l, 1728:1792] = f(lam_k1[l])
        bcast[l, 1792:1856] = f(lam_q2[l])
        bcast[l, 1856:1920] = f(lam_k2[l])
        sguw[l] = f(sgu_w[l]).transpose(2, 0, 1).reshape(128, 512)
    ropeC, ropeS = _rope_tables()
    ident = np.eye(128, dtype=np.float32)
    in_maps = []
    for b in range(B):
        xT = np.ascontiguousarray(np.concatenate([ctx[b], x[b]], axis=0).T)
        cs = np.stack([c[b], c_ctx], axis=1).reshape(8, 128, 2).transpose(1, 0, 2).reshape(128, 16)
        in_maps.append({"xT": xT, "cs": np.ascontiguousarray(cs), "wbig": wbig, "wada": wada, "smallp": smallp,
                        "bcast": bcast, "sguw": sguw, "ropeC": ropeC, "ropeS": ropeS, "ident": ident})
    if "nc" not in _CACHE:
        _CACHE["nc"] = build_program()
    res = run_bass_kernel_spmd(_CACHE["nc"], in_maps, core_ids=list(range(B)))
    out = np.stack([np.ascontiguousarray(r["outT"].T) for r in res.results], axis=0)
    return out.astype(np.float32)
```
